# Optimizing a Trainium2 kernel written in Bass

```python
import math
import jax, jax.numpy as jnp
from jax import lax
import numpy as np

D_MODEL = 1024
BATCH = 2
SEQ = 8192
DEPTH = 2
DEC_BATCH = 32
DEC_SEQ = 4
PAST_LEN = 16384
PAGE_SIZE = 128

N_A_LAYERS = DEPTH // 2
N_B_LAYERS = DEPTH - N_A_LAYERS

POOL_WINDOWS = (2, 4, 8, 16)
N_POOL_GROUPS = len(POOL_WINDOWS)
POOL_CH = D_MODEL // N_POOL_GROUPS
POOL_BUF = max(POOL_WINDOWS) - 1

HEAD_DIM = 64
ATTN_WINDOWS = (128, 512, 2048)
ATTN_DILATIONS = (1, 4, 16)
N_ATTN_GROUPS = len(ATTN_WINDOWS)
KV_PER_GROUP = 2
Q_PER_KV = 3
HEADS_PER_GROUP = KV_PER_GROUP * Q_PER_KV
N_Q_HEADS = N_ATTN_GROUPS * HEADS_PER_GROUP
N_KV_HEADS = N_ATTN_GROUPS * KV_PER_GROUP
Q_WIDTH = N_Q_HEADS * HEAD_DIM
KV_WIDTH = N_KV_HEADS * HEAD_DIM
KV_WINDOW = max(ATTN_WINDOWS)
Q_BLOCK = 128
ROPE_THETA = 10000.0

D_FF = 4 * D_MODEL
EPS = 1e-6

kernel_name = "yoco_pool_dilated_swa_decoder_step"


def _rmsnorm(x, g):
    xf = x.astype(jnp.float32)
    y = xf * lax.rsqrt(jnp.mean(xf * xf, axis=-1, keepdims=True) + EPS) * g.astype(jnp.float32)
    return y.astype(x.dtype)


def _rope(x, pos):
    inv = ROPE_THETA ** (-jnp.arange(0, HEAD_DIM, 2, dtype=jnp.float32) / HEAD_DIM)
    ang = pos.astype(jnp.float32)[:, None] * inv[None, :]
    cos = jnp.cos(ang)[None, :, None, :]
    sin = jnp.sin(ang)[None, :, None, :]
    xf = x.astype(jnp.float32)
    x1, x2 = xf[..., : HEAD_DIM // 2], xf[..., HEAD_DIM // 2:]
    return jnp.concatenate([x1 * cos - x2 * sin, x2 * cos + x1 * sin], axis=-1).astype(x.dtype)


def _pool_mix(u, buf, start_pos, w_pool, pool_scale):
    B, T, D = u.shape
    P = buf.shape[1]
    ue = jnp.concatenate([buf, u], axis=1).astype(jnp.float32)
    cs = jnp.concatenate([jnp.zeros((B, 1, D), jnp.float32), jnp.cumsum(ue, axis=1)], axis=1)
    cs = cs.reshape(B, P + T + 1, N_POOL_GROUPS, POOL_CH)
    hi = P + 1 + jnp.arange(T)
    pos = start_pos + jnp.arange(T)
    means = []
    for g, w in enumerate(POOL_WINDOWS):
        lo = jnp.maximum(hi - w, 0)
        cnt = jnp.minimum(w, pos + 1).astype(jnp.float32)
        means.append((cs[:, hi, g] - cs[:, lo, g]) / cnt[None, :, None])
    mean = jnp.stack(means, axis=2)
    pooled = (mean - u.reshape(B, T, N_POOL_GROUPS, POOL_CH).astype(jnp.float32)).astype(u.dtype)
    y = jnp.einsum('btgc,gce->btge', pooled, w_pool).reshape(B, T, D)
    return y * pool_scale


def _dilated_group_attn(q, k, v, key_offset, window, dilation):
    B, Tq = q.shape[0], q.shape[1]
    qb = math.gcd(Tq, Q_BLOCK)
    nb = Tq // qb
    steps = jnp.arange(window // dilation + 1) * dilation
    scale = HEAD_DIM ** -0.5

    def block(args):
        qi, start = args
        kpos = start + key_offset + jnp.arange(qb)[:, None] - steps[None, :]
        valid = kpos >= 0
        kpos = jnp.maximum(kpos, 0)
        kg = k[:, kpos]
        vg = v[:, kpos]
        s = jnp.einsum('bqgrd,bqkgd->bqgrk', qi, kg, preferred_element_type=jnp.float32) * scale
        s = jnp.where(valid[None, :, None, None, :], s, -jnp.inf)
        lse = jax.nn.logsumexp(s, axis=-1)
        p = jnp.exp(s - lse[..., None]).astype(vg.dtype)
        o = jnp.einsum('bqgrk,bqkgd->bqgrd', p, vg)
        return o, lse

    qs = jnp.swapaxes(q.reshape((B, nb, qb) + q.shape[2:]), 0, 1)
    starts = jnp.arange(nb) * qb
    o, lse = lax.map(block, (qs, starts))
    o = jnp.swapaxes(o, 0, 1).reshape(q.shape)
    lse = jnp.swapaxes(lse, 0, 1).reshape(q.shape[:-1])
    return o, lse


def _dilated_mixer(u, pos, k_all, v_all, key_offset, w_q, w_o):
    B, T, _ = u.shape
    q = _rope((u @ w_q).reshape(B, T, N_Q_HEADS, HEAD_DIM), pos)
    q = q.reshape(B, T, N_ATTN_GROUPS, KV_PER_GROUP, Q_PER_KV, HEAD_DIM)
    outs, lses = [], []
    for g in range(N_ATTN_GROUPS):
        kg = k_all[:, :, g * KV_PER_GROUP:(g + 1) * KV_PER_GROUP]
        vg = v_all[:, :, g * KV_PER_GROUP:(g + 1) * KV_PER_GROUP]
        o, l = _dilated_group_attn(q[:, :, g], kg, vg, key_offset, ATTN_WINDOWS[g], ATTN_DILATIONS[g])
        outs.append(o.reshape(B, T, HEADS_PER_GROUP, HEAD_DIM))
        lses.append(l.reshape(B, T, HEADS_PER_GROUP))
    alpha = jax.nn.softmax(jnp.stack(lses, axis=2), axis=2)
    o = jnp.stack(outs, axis=2) * alpha[..., None].astype(u.dtype)
    return o.reshape(B, T, Q_WIDTH) @ w_o


def _trunk(x, start_pos, pool_bufs, cache_k, cache_v, norm_gains, kv_norm_gain, w_pool, pool_scale,
           w_q, w_o, w_kv, w_up, w_down):
    B, T, _ = x.shape
    pos = start_pos + jnp.arange(T)
    new_pool = []
    k_new = v_new = k_all = v_all = None
    for layer in range(DEPTH):
        g = norm_gains[layer]
        u = _rmsnorm(x, g[0])
        if layer < N_A_LAYERS:
            buf = pool_bufs[layer]
            mix = _pool_mix(u, buf, start_pos, w_pool[layer], pool_scale[layer])
            new_pool.append(jnp.concatenate([buf, u], axis=1)[:, -POOL_BUF:])
        else:
            if layer == N_A_LAYERS:
                kv = _rmsnorm(x, kv_norm_gain) @ w_kv
                k_new = _rope(kv[..., :KV_WIDTH].reshape(B, T, N_KV_HEADS, HEAD_DIM), pos)
                v_new = kv[..., KV_WIDTH:].reshape(B, T, N_KV_HEADS, HEAD_DIM)
                k_all = jnp.concatenate([cache_k, k_new], axis=1)
                v_all = jnp.concatenate([cache_v, v_new], axis=1)
            b = layer - N_A_LAYERS
            mix = _dilated_mixer(u, pos, k_all, v_all, cache_k.shape[1], w_q[b], w_o[b])
        x = x + _rmsnorm(mix, g[1])
        h = _rmsnorm(x, g[2])
        ff = jnp.square(jax.nn.relu(h @ w_up[layer])) @ w_down[layer]
        x = x + _rmsnorm(ff, g[3])
    return x, jnp.stack(new_pool, axis=0), k_new, v_new


def setup_inputs(seed: int = 0) -> dict:
    key = jax.random.key(seed)
    ks = jax.random.split(key, 14)
    f32 = jnp.float32
    kv_buf = min(KV_WINDOW, PAST_LEN)
    nrm = lambda k, s: jax.random.normal(k, s, f32)
    return {
        "x_prompt": nrm(ks[0], (BATCH, SEQ, D_MODEL)),
        "x_sample": nrm(ks[1], (DEC_BATCH, DEC_SEQ, D_MODEL)),
        "cache_pool": nrm(ks[2], (N_A_LAYERS, DEC_BATCH, POOL_BUF, D_MODEL)),
        "cache_k": nrm(ks[3], (DEC_BATCH, kv_buf, N_KV_HEADS, HEAD_DIM)),
        "cache_v": nrm(ks[4], (DEC_BATCH, kv_buf, N_KV_HEADS, HEAD_DIM)),
        "norm_gains": 1.0 + 0.05 * nrm(ks[5], (DEPTH, 4, D_MODEL)),
        "kv_norm_gain": 1.0 + 0.05 * nrm(ks[6], (D_MODEL,)),
        "w_pool": nrm(ks[7], (N_A_LAYERS, N_POOL_GROUPS, POOL_CH, POOL_CH)) * POOL_CH ** -0.5,
        "pool_scale": 1.0 + 0.1 * nrm(ks[8], (N_A_LAYERS, D_MODEL)),
        "w_q": nrm(ks[9], (N_B_LAYERS, D_MODEL, Q_WIDTH)) * D_MODEL ** -0.5,
        "w_o": nrm(ks[10], (N_B_LAYERS, Q_WIDTH, D_MODEL)) * Q_WIDTH ** -0.5,
        "w_kv": nrm(ks[11], (D_MODEL, 2 * KV_WIDTH)) * D_MODEL ** -0.5,
        "w_up": nrm(ks[12], (DEPTH, D_MODEL, D_FF)) * D_MODEL ** -0.5,
        "w_down": nrm(ks[13], (DEPTH, D_FF, D_MODEL)) * D_FF ** -0.5,
    }


def reference(x_prompt, x_sample, cache_pool, cache_k, cache_v, norm_gains, kv_norm_gain, w_pool,
              pool_scale, w_q, w_o, w_kv, w_up, w_down):
    bp = x_prompt.shape[0]
    dt = x_prompt.dtype
    empty_pool = jnp.zeros((N_A_LAYERS, bp, 0, D_MODEL), dt)
    empty_kv = jnp.zeros((bp, 0, N_KV_HEADS, HEAD_DIM), dt)
    y_prompt, pool_prompt, k_p, v_p = _trunk(
        x_prompt, 0, empty_pool, empty_kv, empty_kv, norm_gains, kv_norm_gain, w_pool, pool_scale,
        w_q, w_o, w_kv, w_up, w_down)
    y_sample, pool_sample, k_s, v_s = _trunk(
        x_sample, PAST_LEN, cache_pool, cache_k, cache_v, norm_gains, kv_norm_gain, w_pool, pool_scale,
        w_q, w_o, w_kv, w_up, w_down)
    k_prompt = k_p[:, -KV_WINDOW:]
    v_prompt = v_p[:, -KV_WINDOW:]
    return (y_prompt, y_sample, pool_prompt, k_prompt, v_prompt, pool_sample, k_s, v_s)
```

```python
import math
from contextlib import ExitStack

import numpy as np
import concourse.bass as bass
import concourse.mybir as mybir
from concourse.bass_utils import run_bass_kernel_spmd

F32 = mybir.dt.float32
AF = mybir.ActivationFunctionType
ALU = mybir.AluOpType

D = 1024
NCH = 8
SEQ = 8192
CH = 2048
NT = 512
HD = 64
EPS = 1e-6
PAST = 16384
GRAN = 128
DIL = (1, 4, 16)
DBG = {"tiles": list(range(9)), "pools": True, "emit_y": True}


class Sched:
    ENG = ("pe", "act", "dve", "pool", "sp")

    def __init__(self, nc, stack, n_dma_sems=32):
        self.nc = nc
        self.eng = {"pe": nc.tensor, "act": nc.scalar, "dve": nc.vector, "pool": nc.gpsimd, "sp": nc.sync}
        self.sem = {}
        for e in self.ENG:
            self.sem[e] = stack.enter_context(nc.semaphore("s_" + e))
        self.n_dma = n_dma_sems // 2
        for q in ("sp", "pool"):
            for j in range(self.n_dma):
                self.sem[("d", q, j)] = stack.enter_context(nc.semaphore("s_d%s%d" % (q, j)))
        self.rrq = {"sp": 0, "pool": 0}
        self.cnt = {k: 0 for k in self.sem}
        self.known = {e: {} for e in self.ENG}
        self.last_write = {}
        self.readers = {}
        self.rr = 0
        self.n_wait = 0
        self.n_inst = 0

    def _wait(self, e, tok):
        s, v = tok
        if v <= 0:
            return
        if s == "pe" and e == "pe":
            return
        if self.known[e].get(s, 0) >= v:
            return
        assert v <= self.cnt[s], ("wait on an increment that is not emitted yet", e, tok, self.cnt[s])
        self.eng[e].wait_ge(self.sem[s], v)
        self.known[e][s] = v
        self.n_wait += 1

    def _deps(self, e, reads, writes):
        need = {}
        for r in reads:
            t = self.last_write.get(r)
            if t is not None and need.get(t[0], 0) < t[1]:
                need[t[0]] = t[1]
        for w in writes:
            t = self.last_write.get(w)
            if t is not None and need.get(t[0], 0) < t[1]:
                need[t[0]] = t[1]
            rd = self.readers.get(w)
            if rd:
                for s, v in rd.items():
                    if need.get(s, 0) < v:
                        need[s] = v
        for s, v in need.items():
            self._wait(e, (s, v))

    def _record(self, tok, reads, writes):
        for w in writes:
            self.last_write[w] = tok
            self.readers[w] = {}
        for r in reads:
            d = self.readers.setdefault(r, {})
            if d.get(tok[0], 0) < tok[1]:
                d[tok[0]] = tok[1]

    def op(self, e, fn, reads=(), writes=(), inc=True):
        self._deps(e, reads, writes)
        inst = fn(self.eng[e])
        self.n_inst += 1
        if inc:
            self.cnt[e] += 1
            inst.then_inc(self.sem[e], 1)
            tok = (e, self.cnt[e])
        else:
            tok = (e, self.cnt[e] + 1)
        self._record(tok, reads, writes)
        return tok

    def dma(self, e, out, in_, reads=(), writes=(), **kw):
        j = self.rrq[e]
        self.rrq[e] = (j + 1) % self.n_dma
        key = ("d", e, j)
        self._wait(e, (key, self.cnt[key]))
        self._deps(e, reads, writes)
        inst = self.eng[e].dma_start(out=out, in_=in_, **kw)
        self.n_inst += 1
        self.cnt[key] += 16
        inst.then_inc(self.sem[key], 16)
        tok = (key, self.cnt[key])
        self._record(tok, reads, writes)
        return tok

    def finish(self):
        for e in self.ENG:
            for s in self.sem:
                self._wait(e, (s, self.cnt[s]))


class View:
    def __init__(self, arena, off, d0, d1):
        self.off, self.d0, self.d1 = off, d0, d1
        self.n = d0 * d1
        self.ap = arena[:, off:off + self.n].rearrange("p (a b) -> p a b", a=d0)
        self.flat = arena[:, off:off + self.n]

    def Rr(self, lo, hi):
        a = (self.off + lo) // GRAN
        b = (self.off + hi - 1) // GRAN
        return [("A", g) for g in range(a, b + 1)]

    def Rc(self, c0, c1=None):
        c1 = c0 + 1 if c1 is None else c1
        return self.Rr(c0 * self.d1, c1 * self.d1)

    @property
    def R(self):
        return self.Rr(0, self.n)


def build_program(do_phase_b=True):
    nc = bass.Bass("TRN2", target_bir_lowering=False)

    def din(name, shape):
        return nc.dram_tensor(name, list(shape), F32, kind="ExternalInput").ap()

    def dout(name, shape):
        return nc.dram_tensor(name, list(shape), F32, kind="ExternalOutput").ap()

    xp = din("xp", [2 * CH, D])
    xph = din("xph", [16, D])
    xs = din("xs", [16, D])
    cpool = din("cpool", [60, D])
    ck = din("ck", [4, 2048, 384])
    cv = din("cv", [4, 2048, 384])
    gains_d = din("gains", [128, 64])
    kvg_d = din("kvg", [128, 8])
    pscale_d = din("pscale", [128, 8])
    icnt_d = din("icnt", [128, 256])
    ropec_d = din("ropec", [128, 33 * 32])
    ropes_d = din("ropes", [128, 33 * 32])
    maskA_d = din("maskA", [128, 384])
    maskB_d = din("maskB", [128, 384])
    msamp_d = din("msamp", [128, 7 * 12])
    hmask_d = din("hmask", [128, 4])
    ident_d = din("ident", [128, 128])
    w_pool = din("w_pool", [4, 256, 256])
    w_q = din("w_q", [D, 1152])
    w_o = din("w_o", [1152, D])
    w_kv = din("w_kv", [D, 768])
    w_up = din("w_up", [2, D, 4 * D])
    w_down = din("w_down", [2, 4 * D, D])

    y_o = dout("y", [CH + 16, D])
    kvo = dout("kvo", [CH + 16, 768])
    poolp = dout("poolp", [15, D])
    pools = dout("pools", [4, 15, D])
    kvs = nc.dram_tensor("kvs", [2 * CH + 16, 768], F32, kind="Internal").ap()

    with ExitStack() as st:
        S = Sched(nc, st)

        def sb(name, shape):
            return st.enter_context(nc.sbuf_tensor(name, list(shape), F32))

        xres = sb("xres", [128, NCH, CH])
        xsres = sb("xsres", [128, NCH, 16])
        ident = sb("ident_sb", [128, 128])
        onesm = sb("onesm", [128, 128])
        gains = sb("gains_sb", [128, 64])
        kvg = sb("kvg_sb", [128, 8])
        pscale = sb("pscale_sb", [128, 8])
        icnt = sb("icnt_sb", [128, 2, 8, 16])
        ropec = sb("ropec_sb", [128, 33, 32])
        ropes = sb("ropes_sb", [128, 33, 32])
        maskA = sb("maskA_sb", [128, 3, 128])
        maskB = sb("maskB_sb", [128, 3, 128])
        msamp = sb("msamp_sb", [128, 7, 3, 4])
        hmask = sb("hmask_sb", [128, 4])
        epsb = sb("epsb", [128, 1])
        acc = sb("acc", [128, NT])
        sq = [sb("sq%d" % i, [128, NT]) for i in range(2)]
        rs = [sb("rs%d" % i, [128, NT]) for i in range(2)]
        halo = sb("halo", [128, NCH, 15])
        uexs = sb("uexs", [128, NCH, 76])
        kvst_t = [sb("kvst%d" % i, [128, 768]) for i in range(2)]
        hs = sb("hs", [128, NCH, 16])
        hids = sb("hids", [128, 32, 16])
        ffs = sb("ffs", [128, NCH, 16])
        deferred = []

        def drain(n=None):
            k = len(deferred) if n is None else min(n, len(deferred))
            for _ in range(k):
                deferred.pop(0)()

        preloaded = set()
        ARENA_N = 27136
        arena = sb("arena", [128, ARENA_N])
        psb = [st.enter_context(nc.psum_tensor("psb%d" % i, [128, 512], F32)) for i in range(8)]

        state = {"ps": 0, "rs": 0, "wb": 0, "kb": 0, "pb": 0, "tmp": 0}

        def psum():
            i = state["ps"]
            state["ps"] = (i + 1) % 6
            return psb[i], ("ps", i)

        UB = View(arena, 0, NCH, NT + 15)
        OT = View(arena, 0, 9, NT)
        MT = View(arena, 4608, NCH, NT)
        HID = View(arena, 8704, 16, NT)
        WB = [View(arena, 16896, NCH, NT), View(arena, 20992, NCH, NT)]
        WPOOL = View(arena, 25088, 8, 256)
        DT = View(arena, 25088, 3, NT)
        XB = 27136
        XTOK = [View(arena, 8704 + 1024 * i, 1, 1024) for i in range(2)]
        PA = View(arena, 8704 + 2048, 2, NT + 15)
        PBv = View(arena, 8704 + 3584, 2, NT + 15)
        PL = [View(arena, 8704 + 5120 + 1024 * i, 2, NT) for i in range(2)]
        TMP = [View(arena, 8704 + 7168 + 512 * i, 1, NT) for i in range(2)]
        KVST = [View(arena, 4608 + 1024 * i, 1, 768) for i in range(2)]
        QT = View(arena, 8704, 9, NT)
        QST = View(arena, 8704 + 4608, 1, 1152)
        KBLK = [View(arena, 8704 + 5760 + 128 * i, 1, 128) for i in range(4)]
        KT = [View(arena, 8704 + 6272 + 128 * i, 1, 128) for i in range(4)]
        VA = [View(arena, 8704 + 6784 + 256 * i, 2, 128) for i in range(4)]
        PB = [View(arena, 8704 + 7808, 3, 128)]
        OSN = View(arena, 4608 + 2048, 6, 64)
        OSD = View(arena, 4608 + 2048 + 384, 6, 64)
        for i in range(3):
            PB.append(View(arena, 4608 + 2048 + 768 + 384 * i, 3, 128))
        YTOK = [View(arena, 4608 + 1024 * i, 1, 1024) for i in range(2)]

        def xr_regs(slot):
            return [("xres", slot)]

        for dst, src, nm in ((ident, ident_d, "ident"), (gains, gains_d, "gains"), (kvg, kvg_d, "kvg"),
                             (pscale, pscale_d, "pscale"), (hmask, hmask_d, "hmask")):
            S.dma("pool", dst[:], src, writes=[nm])
        S.dma("pool", icnt[:].rearrange("p a c t -> p (a c t)"), icnt_d, writes=["icnt"])
        S.dma("pool", ropec[:].rearrange("p a b -> p (a b)"), ropec_d, writes=["ropec"])
        S.dma("pool", ropes[:].rearrange("p a b -> p (a b)"), ropes_d, writes=["ropes"])
        S.dma("pool", maskA[:].rearrange("p a b -> p (a b)"), maskA_d, writes=["masks"])
        S.dma("pool", maskB[:].rearrange("p a b -> p (a b)"), maskB_d, writes=["masks"])
        S.dma("pool", msamp[:].rearrange("p a b c -> p (a b c)"), msamp_d, writes=["masks"])
        S.op("dve", lambda e: e.memset(onesm[:], 1.0 / D), writes=["onesm"])
        S.op("dve", lambda e: e.memset(epsb[:], EPS), writes=["epsb"])

        def load_xT(src_rows, ntok, dst_fn, dst_regs):
            nblk = (ntok + 127) // 128
            for b in range(nblk):
                nb = min(128, ntok - b * 128)
                xt = XTOK[b % 2]
                S.dma("sp", xt.flat[:nb, :], src_rows[b * 128:b * 128 + nb, :], writes=xt.R)
                for half in range(2):
                    ps, pr = psum()
                    for cc in range(4):
                        c = half * 4 + cc
                        S.op("pe", lambda e, ps=ps, cc=cc, c=c, xt=xt, nb=nb: e.transpose(
                            out=ps[:, cc * 128:cc * 128 + nb], in_=xt.flat[:nb, c * 128:(c + 1) * 128],
                            identity=ident[:nb, :nb]), reads=xt.R + ["ident"], writes=[pr], inc=(cc == 3))
                    S.op("act", lambda e, ps=ps, half=half, b=b, nb=nb: e.activation(
                        out=dst_fn(half * 4, half * 4 + 4, b * 128, nb),
                        in_=ps[:, :].rearrange("p (a b) -> p a b", a=4)[:, :, :nb], func=AF.Copy),
                        reads=[pr], writes=dst_regs)

        def ubreg(c, N):
            return UB.Rr(c * (NT + 15) + 15, c * (NT + 15) + 15 + N)

        def scr_blk(c0, c1):
            return UB.ap[:, c0:c1, 15:15 + NT]

        def scr_R(c0, c1):
            return UB.Rr(c0 * (NT + 15) + 15, (c1 - 1) * (NT + 15) + 15 + NT)

        def rstd_from(sum_ap, sum_regs, N):
            ps, pr = psum()
            S.op("pe", lambda e: e.matmul(ps[:, :N], lhsT=onesm[:], rhs=sum_ap, start=True, stop=True),
                 reads=sum_regs + ["onesm"], writes=[pr])
            k = state["rs"]
            state["rs"] = 1 - k
            r = rs[k]
            S.op("act", lambda e: e.activation(out=r[:, :N], in_=ps[:, :N], func=AF.Sqrt, bias=epsb[:, 0:1], scale=1.0),
                 reads=[pr, "epsb"], writes=[("rs", k)])
            S.op("dve", lambda e: e.reciprocal(out=r[:, :N], in_=r[:, :N]), reads=[("rs", k)], writes=[("rs", k)])
            return r, ("rs", k)

        def rms_rstd(src_fn, sregs, N, src_blk=None):
            if N == NT and src_blk is not None:
                S.op("act", lambda e: e.activation(out=scr_blk(0, 4), in_=src_blk(0, 4), func=AF.Square),
                     reads=sregs, writes=scr_R(0, 4))
                S.op("dve", lambda e: e.tensor_tensor(out=scr_blk(4, 8), in0=src_blk(4, 8), in1=src_blk(4, 8), op=ALU.mult),
                     reads=sregs, writes=scr_R(4, 8))
                for a, b in ((4, 8), (2, 4), (1, 2)):
                    w = b - a
                    S.op("dve", lambda e, a=a, b=b, w=w: e.tensor_tensor(out=scr_blk(0, w), in0=scr_blk(0, w),
                                                                       in1=scr_blk(a, b), op=ALU.add),
                         reads=scr_R(0, b), writes=scr_R(0, w))
                return rstd_from(UB.ap[:, 0, 15:15 + NT], scr_R(0, 1), N)
            for c in range(NCH):
                if c == 0:
                    S.op("act", lambda e: e.activation(out=acc[:, :N], in_=src_fn(0), func=AF.Square),
                         reads=sregs, writes=["acc"])
                else:
                    sqb = sq[c % 2]
                    S.op("act", lambda e, sqb=sqb, c=c: e.activation(out=sqb[:, :N], in_=src_fn(c), func=AF.Square),
                         reads=sregs, writes=[("sq", c % 2)])
                    S.op("pool", lambda e, sqb=sqb: e.tensor_tensor(out=acc[:, :N], in0=acc[:, :N], in1=sqb[:, :N],
                                                                    op=ALU.add),
                         reads=["acc", ("sq", c % 2)], writes=["acc"])
            return rstd_from(acc[:, :N], ["acc"], N)

        def rms_to(src_fn, sregs, gcol, gname, dst_fn, dregs, N, src_blk=None):
            r, rr = rms_rstd(src_fn, sregs, N, src_blk)
            for c in range(NCH):
                dr = dregs(c) if callable(dregs) else dregs
                S.op("dve", lambda e, c=c: e.scalar_tensor_tensor(
                    out=dst_fn(c), in0=src_fn(c), scalar=gcol(c), in1=r[:, :N], op0=ALU.mult, op1=ALU.mult),
                    reads=sregs + [rr, gname], writes=dr)

        def rms_residual(src_fn, sregs, gcol, gname, x_fn, xregs, N, src_blk=None):
            r, rr = rms_rstd(src_fn, sregs, N, src_blk)
            for c in range(NCH):
                if N == NT and src_blk is not None:
                    tb_ap, tb_r = UB.ap[:, c, 15:15 + NT], ubreg(c, NT)
                else:
                    k = state["tmp"]
                    state["tmp"] = 1 - k
                    tb_ap, tb_r = sq[k][:, :N], [("sq", k)]
                S.op("dve", lambda e, c=c, tb_ap=tb_ap: e.scalar_tensor_tensor(
                    out=tb_ap, in0=src_fn(c), scalar=gcol(c), in1=r[:, :N], op0=ALU.mult, op1=ALU.mult),
                    reads=sregs + [rr, gname], writes=tb_r)
                S.op("pool" if c % 3 != 2 else "dve",
                     lambda e, c=c, tb_ap=tb_ap: e.tensor_tensor(out=x_fn(c), in0=x_fn(c), in1=tb_ap, op=ALU.add),
                     reads=xregs + tb_r, writes=xregs)

        def gcol_fn(n):
            return lambda c: gains[:, n * 8 + c:n * 8 + c + 1]

        def next_wb():
            k = state["wb"]
            state["wb"] = 1 - k
            return WB[k]

        def mlp(layer, h_fn, hregs_fn, N, extra=False):
            wu = w_up[layer].rearrange("(kc p) h -> p kc h", p=128)
            wd = w_down[layer].rearrange("(kc p) f -> p kc f", p=128)
            for half in range(2):
                for jg in range(4):
                    wb = next_wb()
                    h0 = half * 2048 + jg * 512
                    S.dma("sp", wb.ap, wu[:, :, h0:h0 + 512], writes=wb.R)
                    for jj in range(4):
                        j = jg * 4 + jj
                        ps, pr = psum()
                        for kc in range(NCH):
                            S.op("pe", lambda e, ps=ps, wb=wb, kc=kc, jj=jj: e.matmul(
                                ps[:, :N], lhsT=wb.ap[:, kc, jj * 128:(jj + 1) * 128], rhs=h_fn(kc),
                                start=(kc == 0), stop=(kc == NCH - 1)),
                                reads=wb.Rc(kc) + hregs_fn(kc), writes=[pr], inc=(kc == NCH - 1))
                        S.op("act", lambda e, ps=ps, j=j: e.activation(out=HID.ap[:, j, :N], in_=ps[:, :N], func=AF.Relu),
                             reads=[pr], writes=HID.Rc(j))
                        S.op("dve", lambda e, j=j: e.tensor_tensor(out=HID.ap[:, j, :N], in0=HID.ap[:, j, :N],
                                                                    in1=HID.ap[:, j, :N], op=ALU.mult),
                             reads=HID.Rc(j), writes=HID.Rc(j))
                    if extra:
                        ps, pr = psum()
                        for jj in range(4):
                            for kc in range(NCH):
                                S.op("pe", lambda e, ps=ps, wb=wb, kc=kc, jj=jj: e.matmul(
                                    ps[:, jj * 16:(jj + 1) * 16], lhsT=wb.ap[:, kc, jj * 128:(jj + 1) * 128], rhs=hs[:, kc, :],
                                    start=(jj == 0 and kc == 0), stop=(jj == 3 and kc == NCH - 1)),
                                    reads=wb.Rc(kc) + ["hs"], writes=[pr], inc=(jj == 3 and kc == NCH - 1))
                        jb = half * 16 + jg * 4
                        hv = hids[:, jb:jb + 4, :]
                        S.op("act", lambda e, ps=ps, hv=hv: e.activation(
                            out=hv, in_=ps[:, 0:64].rearrange("p (a b) -> p a b", a=4), func=AF.Relu),
                            reads=[pr], writes=["hids"])
                        S.op("dve", lambda e, hv=hv: e.tensor_tensor(out=hv, in0=hv, in1=hv, op=ALU.mult),
                             reads=["hids"], writes=["hids"])
                for mh in range(2):
                    accs = [psum() for _ in range(4)]
                    xacc = psum() if extra else None
                    for kcg in range(4):
                        wb = next_wb()
                        k0 = half * 16 + kcg * 4
                        S.dma("sp", wb.ap[:, 0:4, :], wd[:, k0:k0 + 4, mh * 512:(mh + 1) * 512], writes=wb.Rc(0, 4))
                        for kc in range(4):
                            for m in range(4):
                                ps, pr = accs[m]
                                first = (kcg == 0 and kc == 0)
                                last = (kcg == 3 and kc == 3)
                                S.op("pe", lambda e, ps=ps, wb=wb, kc=kc, m=m, kcg=kcg, first=first, last=last: e.matmul(
                                    ps[:, :N], lhsT=wb.ap[:, kc, m * 128:(m + 1) * 128], rhs=HID.ap[:, kcg * 4 + kc, :N],
                                    start=first, stop=last),
                                    reads=wb.Rc(kc) + HID.Rc(kcg * 4 + kc), writes=[pr],
                                    inc=(last or (kc == 3 and m == 3 and not extra)))
                        if extra:
                            ps, pr = xacc
                            for kc in range(4):
                                for m in range(4):
                                    first = (kcg == 0 and kc == 0 and m == 0)
                                    last = (kcg == 3 and kc == 3 and m == 3)
                                    S.op("pe", lambda e, ps=ps, wb=wb, kc=kc, m=m, first=first, last=last: e.matmul(
                                        ps[:, m * 16:(m + 1) * 16], lhsT=wb.ap[:, kc, m * 128:(m + 1) * 128],
                                        rhs=hids[:, k0 + kc, :], start=first, stop=last),
                                        reads=wb.Rc(kc) + ["hids"], writes=[pr], inc=(kc == 3 and m == 3))
                    for m in range(4):
                        ps, pr = accs[m]
                        c = mh * 4 + m
                        if half == 0:
                            S.op("act", lambda e, ps=ps, c=c: e.activation(out=MT.ap[:, c, :N], in_=ps[:, :N], func=AF.Copy),
                                 reads=[pr], writes=MT.Rc(c))
                        else:
                            S.op("dve", lambda e, ps=ps, c=c: e.tensor_tensor(out=MT.ap[:, c, :N], in0=ps[:, :N],
                                                                             in1=MT.ap[:, c, :N], op=ALU.add),
                                 reads=[pr] + MT.Rc(c), writes=MT.Rc(c))
                    if extra:
                        ps, pr = xacc
                        fv = ffs[:, mh * 4:mh * 4 + 4, :]
                        pv = ps[:, 0:64].rearrange("p (a b) -> p a b", a=4)
                        if half == 0:
                            S.op("act", lambda e: e.activation(out=fv, in_=pv, func=AF.Copy), reads=[pr], writes=["ffs"])
                        else:
                            S.op("dve", lambda e: e.tensor_tensor(out=fv, in0=pv, in1=fv, op=ALU.add),
                                 reads=[pr, "ffs"], writes=["ffs"])

        def rope_evac(ps, pr, nb, blk, dst, dregs, perm=False):
            k = state["tmp"]
            state["tmp"] = 1 - k
            tb = sq[k]
            if perm:
                pv = ps[:nb, 0:384].rearrange("p (kv r d) -> p kv r d", kv=2, r=3)
                cb = ropec[:nb, blk, :].unsqueeze(1).unsqueeze(1).broadcast_to([nb, 2, 3, 32])
                sbb = ropes[:nb, blk, :].unsqueeze(1).unsqueeze(1).broadcast_to([nb, 2, 3, 32])
                tv = tb[:nb, 0:192].rearrange("p (kv r d) -> p kv r d", kv=2, r=3)
                lo = lambda a: a[:, :, :, 0:32]
                hi = lambda a: a[:, :, :, 32:64]
            else:
                pv = ps[:nb, 0:384].rearrange("p (h d) -> p h d", h=6)
                cb = ropec[:nb, blk, :].unsqueeze(1).broadcast_to([nb, 6, 32])
                sbb = ropes[:nb, blk, :].unsqueeze(1).broadcast_to([nb, 6, 32])
                tv = tb[:nb, 0:192].rearrange("p (h d) -> p h d", h=6)
                lo = lambda a: a[:, :, 0:32]
                hi = lambda a: a[:, :, 32:64]
            rd = [pr, "ropec", "ropes"]
            S.op("dve", lambda e: e.tensor_tensor(out=lo(dst), in0=lo(pv), in1=cb, op=ALU.mult), reads=rd, writes=dregs)
            S.op("dve", lambda e: e.tensor_tensor(out=tv, in0=hi(pv), in1=sbb, op=ALU.mult), reads=rd, writes=[("sq", k)])
            S.op("dve", lambda e: e.tensor_tensor(out=lo(dst), in0=lo(dst), in1=tv, op=ALU.subtract),
                 reads=dregs + [("sq", k)], writes=dregs)
            S.op("dve", lambda e: e.tensor_tensor(out=hi(dst), in0=hi(pv), in1=cb, op=ALU.mult), reads=rd, writes=dregs)
            S.op("dve", lambda e: e.tensor_tensor(out=tv, in0=lo(pv), in1=sbb, op=ALU.mult),
                 reads=rd + [("sq", k)], writes=[("sq", k)])
            S.op("dve", lambda e: e.tensor_tensor(out=hi(dst), in0=hi(dst), in1=tv, op=ALU.add),
                 reads=dregs + [("sq", k)], writes=dregs)

        def phase_a_tile(ti, stage="all", extra=False):
            sample = (ti == 8)
            N = 16 if sample else NT
            if sample:
                x_fn = lambda c: xsres[:, c, :]
                xregs = ["xsres"]
                x_dst = lambda c0, c1, t0, nb: xsres[:, c0:c1, t0:t0 + nb]
                src_rows = xs
                x_blk = None
                mt_blk = None
            else:
                slot = ti % 4
                x_fn = lambda c: xres[:, c, slot * NT:(slot + 1) * NT]
                xregs = xr_regs(slot)
                x_dst = lambda c0, c1, t0, nb: xres[:, c0:c1, slot * NT + t0:slot * NT + t0 + nb]
                src_rows = xp[ti * NT:(ti + 1) * NT, :]
                x_blk = lambda c0, c1: xres[:, c0:c1, slot * NT:(slot + 1) * NT]
                mt_blk = lambda c0, c1: MT.ap[:, c0:c1, :]
            ubr = lambda c: ubreg(c, N)
            nb_fn = lambda c: UB.ap[:, c, 15:15 + N]
            if stage != "post":
                if ti == 0:
                    load_xT(xph, 16, lambda c0, c1, t0, nb: uexs[:, c0:c1, 0:16], ["uexs"])
                    rms_to(lambda c: uexs[:, c, 0:16], ["uexs"], gcol_fn(0), "gains",
                           lambda c: uexs[:, c, 16:32], ["uexs"], 16)
                    S.op("pool", lambda e: e.tensor_copy(out=UB.ap[:, :, 0:15], in_=uexs[:, :, 17:32]),
                         reads=["uexs"], writes=UB.R)
                elif not sample:
                    S.op("pool", lambda e: e.tensor_copy(out=UB.ap[:, :, 0:15], in_=halo[:]), reads=["halo"], writes=UB.R)
                drain(1)
                if ti not in preloaded:
                    load_xT(src_rows, N, x_dst, xregs)
                    preloaded.add(ti)
                if DBG.get('stop', 99) <= 1:
                    return
                if sample:
                    load_xT(cpool, 60, lambda c0, c1, t0, nb: uexs[:, c0:c1, :].rearrange(
                        "p c (s t) -> p c s t", s=4)[:, :, :, 0:15], ["uexs"])
                    rms_to(x_fn, xregs, gcol_fn(0), "gains", nb_fn, UB.R, N)
                    S.op("pool", lambda e: e.tensor_copy(
                        out=uexs[:, :, :].rearrange("p c (s t) -> p c s t", s=4)[:, :, :, 15:19],
                        in_=UB.ap[:, :, 15:31].rearrange("p c (s t) -> p c s t", s=4)), reads=UB.R, writes=["uexs"])
                    uext = lambda c0, c1: uexs[:, c0:c1, :]
                    uregs = ["uexs"]
                    E = 76
                else:
                    rms_to(x_fn, xregs, gcol_fn(0), "gains", nb_fn, ubr, N, x_blk)
                    drain(1)
                    S.op("pool", lambda e: e.tensor_copy(out=halo[:], in_=UB.ap[:, :, NT:NT + 15]), reads=UB.R, writes=["halo"])
                    uext = lambda c0, c1: UB.ap[:, c0:c1, :]
                    uregs = UB.R
                    E = NT + 15
                if DBG.get('stop', 99) <= 2:
                    return
                if ti == 7 or sample:
                    drain()
                if ti == 7:
                    for half in range(2):
                        ps, pr = psum()
                        for cc in range(4):
                            c = half * 4 + cc
                            S.op("pe", lambda e, ps=ps, cc=cc, c=c: e.transpose(
                                out=ps[:15, cc * 128:(cc + 1) * 128], in_=UB.ap[:, c, NT:NT + 15], identity=ident[:, :]),
                                reads=UB.R + ["ident"], writes=[pr], inc=(cc == 3))
                        S.op("act", lambda e, ps=ps, half=half: e.activation(
                            out=YTOK[0].flat[:15, half * 512:(half + 1) * 512], in_=ps[:15, :], func=AF.Copy),
                            reads=[pr], writes=YTOK[0].R)
                    S.dma("pool", poolp, YTOK[0].flat[:15, :], reads=YTOK[0].R, writes=["poolp"])
                if sample:
                    for half in range(2):
                        ps, pr = psum()
                        for cc in range(4):
                            c = half * 4 + cc
                            S.op("pe", lambda e, ps=ps, cc=cc, c=c: e.transpose(
                                out=ps[:16, cc * 128:(cc + 1) * 128],
                                in_=UB.ap[:, c, 15:31], identity=ident[:, :]),
                                reads=UB.R + ["ident"], writes=[pr], inc=(cc == 3))
                        S.op("act", lambda e, ps=ps, half=half: e.activation(
                            out=YTOK[0].flat[:16, half * 512:(half + 1) * 512], in_=ps[:16, :], func=AF.Copy),
                            reads=[pr], writes=YTOK[0].R)
                    for s in range(4):
                        S.dma("pool", pools[s, 0:11, :], cpool[s * 15 + 4:s * 15 + 15, :], writes=[("pools", s)])
                        S.dma("pool", pools[s, 11:15, :], YTOK[0].flat[s * 4:(s + 1) * 4, :], reads=YTOK[0].R,
                              writes=[("pools", s)])
                if DBG.get('stop', 99) <= 3:
                    return
                for g in range(4):
                    w = 2 << g
                    src = uext(2 * g, 2 * g + 2)
                    sreg = uregs
                    bufs = [PA, PBv]
                    for k in range(g + 1):
                        sh = 1 << k
                        lo = (2 << k) - 1
                        dv = bufs[k % 2]
                        dst = dv.ap[:, :, 0:E]
                        S.op("dve", lambda e, dst=dst, src=src, lo=lo, sh=sh: e.tensor_tensor(
                            out=dst[:, :, lo:E], in0=src[:, :, lo:E], in1=src[:, :, lo - sh:E - sh], op=ALU.add),
                            reads=sreg, writes=dv.R)
                        src, sreg = dst, dv.R
                    pl = PL[g % 2]
                    u2 = uext(2 * g, 2 * g + 2)
                    if sample:
                        sv = src.rearrange("p c (s t) -> p c s t", s=4)
                        uv = u2.rearrange("p c (s t) -> p c s t", s=4)
                        for cc in range(2):
                            S.op("dve", lambda e, cc=cc: e.scalar_tensor_tensor(
                                out=pl.ap[:, cc, 0:16].rearrange("p (s t) -> p s t", s=4), in0=sv[:, cc, :, 15:19],
                                scalar=1.0 / w, in1=uv[:, cc, :, 15:19], op0=ALU.mult, op1=ALU.subtract),
                                reads=sreg + uregs, writes=pl.R)
                    else:
                        S.op("dve", lambda e: e.scalar_tensor_tensor(
                            out=pl.ap[:, :, :], in0=src[:, :, 15:E], scalar=1.0 / w, in1=u2[:, :, 15:E],
                            op0=ALU.mult, op1=ALU.subtract), reads=sreg + uregs, writes=pl.R)
                        if ti in (0, 4):
                            wh = 0 if ti == 0 else 1
                            S.op("dve", lambda e: e.tensor_tensor(out=pl.ap[:, :, 0:16], in0=src[:, :, 15:31],
                                                                  in1=icnt[:, wh, 2 * g:2 * g + 2, :], op=ALU.mult),
                                 reads=sreg + ["icnt"], writes=pl.R)
                            S.op("dve", lambda e: e.tensor_tensor(out=pl.ap[:, :, 0:16], in0=pl.ap[:, :, 0:16],
                                                                  in1=u2[:, :, 15:31], op=ALU.subtract),
                                 reads=pl.R + uregs, writes=pl.R)
                    drain()
                    for eo in range(2):
                        ps, pr = psum()
                        for cc in range(2):
                            S.op("pe", lambda e, ps=ps, cc=cc, eo=eo: e.matmul(
                                ps[:, :N], lhsT=WPOOL.ap[:, g * 2 + cc, eo * 128:(eo + 1) * 128], rhs=pl.ap[:, cc, :N],
                                start=(cc == 0), stop=(cc == 1)),
                                reads=WPOOL.R + pl.R, writes=[pr], inc=(cc == 1))
                        c = 2 * g + eo
                        S.op("act", lambda e, ps=ps, c=c: e.activation(out=MT.ap[:, c, :N], in_=ps[:, :N], func=AF.Copy,
                                                                        scale=pscale[:, c:c + 1]),
                             reads=[pr, "pscale"], writes=MT.Rc(c))
                if DBG.get('stop', 99) <= 4:
                    return
                mt_fn = lambda c: MT.ap[:, c, :N]
                tl = DBG["tiles"]
                nxt = tl[tl.index(ti) + 1] if tl.index(ti) + 1 < len(tl) else None
                if nxt is not None and nxt not in preloaded:
                    if nxt == 8:
                        load_xT(xs, 16, lambda c0, c1, t0, nb: xsres[:, c0:c1, t0:t0 + nb], ["xsres"])
                    else:
                        ns = nxt % 4
                        load_xT(xp[nxt * NT:(nxt + 1) * NT, :], NT,
                                lambda c0, c1, t0, nb, ns=ns: xres[:, c0:c1, ns * NT + t0:ns * NT + t0 + nb], xr_regs(ns))
                    preloaded.add(nxt)
                rms_residual(mt_fn, MT.R, gcol_fn(1), "gains", x_fn, xregs, N, mt_blk)
                if DBG.get('stop', 99) <= 5:
                    return
            mt_fn = lambda c: MT.ap[:, c, :N]
            if stage == "pre":
                rms_to(x_fn, xregs, gcol_fn(2), "gains", lambda c: hs[:, c, :], ["hs"], N)
                return
            if stage == "post":
                drain()
                rms_residual(lambda c: ffs[:, c, :], ["ffs"], gcol_fn(3), "gains", x_fn, xregs, N)
            else:
                rms_to(x_fn, xregs, gcol_fn(2), "gains", nb_fn, ubr, N, x_blk)
                if DBG.get('stop', 99) <= 6:
                    return
                mlp(0, nb_fn, ubr, N, extra=extra)
                if DBG.get('stop', 99) <= 7:
                    return
                rms_residual(mt_fn, MT.R, gcol_fn(3), "gains", x_fn, xregs, N, mt_blk)
            if DBG.get('stop', 99) <= 8:
                return
            rms_to(x_fn, xregs, lambda c: kvg[:, c:c + 1], "kvg", mt_fn, lambda c: MT.Rc(c), N, x_blk)
            wkv = w_kv.rearrange("(kc p) f -> p kc f", p=128)
            wbs = []
            for part in range(2):
                wb = next_wb()
                S.dma("sp", wb.ap[:, :, 0:384], wkv[:, :, part * 384:(part + 1) * 384], writes=wb.R)
                wbs.append(wb)
            nblk = (N + 127) // 128

            def kv_block(b):
                nb = min(128, N - b * 128)
                kst = kvst_t[b % 2]
                kreg = [("kvst", b % 2)]
                pss = []
                for part in range(2):
                    ps, pr = psum()
                    wb = wbs[part]
                    for kc in range(NCH):
                        S.op("pe", lambda e, ps=ps, wb=wb, kc=kc: e.matmul(
                            ps[:nb, 0:384], lhsT=MT.ap[:, kc, b * 128:b * 128 + nb], rhs=wb.ap[:, kc, 0:384],
                            start=(kc == 0), stop=(kc == NCH - 1)),
                            reads=MT.Rc(kc) + wb.Rc(kc), writes=[pr], inc=(kc == NCH - 1))
                    pss.append((ps, pr))
                blk = 32 if sample else ti * 4 + b
                rope_evac(pss[0][0], pss[0][1], nb, blk, kst[:nb, 0:384].rearrange("p (h d) -> p h d", h=6), kreg)
                S.op("act", lambda e: e.activation(out=kst[:nb, 384:768], in_=pss[1][0][:nb, 0:384], func=AF.Copy),
                     reads=[pss[1][1]], writes=kreg)
                if sample:
                    S.dma("pool", kvs[2 * CH:2 * CH + 16, :], kst[:16, :], reads=kreg, writes=[("kvs", 8)])
                    S.dma("pool", kvo[CH:CH + 16, :], kst[:16, :], reads=kreg, writes=["kvo"])
                else:
                    r0 = ti * NT + b * 128
                    S.dma("pool", kvs[r0:r0 + 128, :], kst[:, :], reads=kreg, writes=[("kvs", ti)])
                    if ti >= 4:
                        S.dma("pool", kvo[r0 - CH:r0 - CH + 128, :], kst[:, :], reads=kreg, writes=["kvo"])

            for b in range(nblk):
                deferred.append(lambda b=b: kv_block(b))

        for g in range(4):
            S.dma("sp", WPOOL.ap[:, g * 2:g * 2 + 2, :], w_pool[g].rearrange("(cc p) e -> p cc e", p=128),
                  writes=WPOOL.Rc(g * 2, g * 2 + 2))
        tl_a = list(DBG["tiles"])
        merge_a = (7 in tl_a and 8 in tl_a)
        for ti in tl_a:
            if merge_a and ti == 7:
                phase_a_tile(8, stage="pre")
                phase_a_tile(7, extra=True)
            elif merge_a and ti == 8:
                phase_a_tile(8, stage="post")
            else:
                phase_a_tile(ti)
        drain()

        def emit_y(blocks):
            for b in blocks:
                nb = 128 if b < 16 else 16
                yt = YTOK[b % 2]
                for half in range(2):
                    ps, pr = psum()
                    for cc in range(4):
                        c = half * 4 + cc
                        src = xres[:, c, b * 128:(b + 1) * 128] if b < 16 else xsres[:, c, :]
                        rd = [("xres", b // 4)] if b < 16 else ["xsres"]
                        S.op("pe", lambda e, ps=ps, cc=cc, src=src, nb=nb: e.transpose(
                            out=ps[:nb, cc * 128:(cc + 1) * 128], in_=src, identity=ident[:, :]),
                            reads=rd + ["ident"], writes=[pr], inc=(cc == 3))
                    S.op("act", lambda e, ps=ps, half=half, nb=nb, yt=yt: e.activation(
                        out=yt.flat[:nb, half * 512:(half + 1) * 512], in_=ps[:nb, :], func=AF.Copy),
                        reads=[pr], writes=yt.R)
                S.dma("pool", y_o[b * 128:b * 128 + nb, :], yt.flat[:nb, :], reads=yt.R, writes=["y"])

        pso = [(psb[6], ("ps", 6)), (psb[7], ("ps", 7))]

        def attn_block(g, nq, colsel, kblocks):
            if DBG.get('nonew') and kblocks[-1]['nk'] == 4:
                kblocks = kblocks[:-1]
            if g not in DBG.get('agroups', (0, 1, 2)):
                return
            nkb = len(kblocks)
            AS = DBG.get('astop', 99)
            for p0 in range(0, nkb, 2):
                grp = list(enumerate(kblocks))[p0:p0 + 2]
                bufs = {}
                for bi, kbk in grp:
                    i = state["kb"]
                    state["kb"] = (i + 1) % 4
                    kb, kt, va = KBLK[i], KT[i], VA[i]
                    bufs[bi] = (kt, va)
                    nk = kbk["nk"]
                    S.dma("pool", kb.flat[:nk, :], kbk["k_src"], reads=kbk["reads"], writes=kb.R)
                    S.dma("pool", va.ap[:nk, :, 0:64], kbk["v_src"].rearrange("n (h d) -> n h d", h=2),
                          reads=kbk["reads"], writes=va.R)
                    pst, ptr = psum()
                    S.op("pe", lambda e: e.transpose(out=pst[:, :nk], in_=kb.flat[:nk, :], identity=ident[:nk, :nk]),
                         reads=kb.R + ["ident"], writes=[ptr])
                    S.op("act", lambda e: e.activation(out=kt.flat[:, :nk], in_=pst[:, :nk], func=AF.Copy),
                         reads=[ptr], writes=kt.R)
                svs = {}
                for bi, kbk in grp:
                    kt, va = bufs[bi]
                    nk = kbk["nk"]
                    for kv in range(2):
                        pss, psr = psum()
                        sv = pss[:nk, 0:3 * nq].rearrange("p (r q) -> p r q", r=3)
                        S.op("pe", lambda e: e.matmul(sv, lhsT=kt.flat[kv * 64:(kv + 1) * 64, :nk],
                                                      rhs=colsel(QT.ap[kv * 64:(kv + 1) * 64, 3 * g:3 * g + 3, :]),
                                                      start=True, stop=True),
                             reads=kt.R + QT.Rc(3 * g, 3 * g + 3), writes=[psr])
                        svs[(bi, kv)] = (sv, psr)
                pbs = {}
                for bi, kbk in grp:
                    nk = kbk["nk"]
                    for kv in range(2):
                        sv, psr = svs[(bi, kv)]
                        j = state["pb"]
                        state["pb"] = (j + 1) % 4
                        pb = PB[j]
                        pv = pb.ap[:nk, :, 0:nq]
                        S.op("act", lambda e: e.activation(out=pv, in_=sv, func=AF.Exp, scale=0.125),
                             reads=[psr], writes=pb.R)
                        mk = kbk["mask"]
                        if kbk["hm"] is None:
                            S.op("dve", lambda e: e.tensor_tensor(out=pv, in0=pv, in1=mk, op=ALU.mult),
                                 reads=pb.R + ["masks"], writes=pb.R)
                        else:
                            S.op("dve", lambda e: e.scalar_tensor_tensor(out=pv, in0=pv, scalar=kbk["hm"], in1=mk,
                                                                          op0=ALU.mult, op1=ALU.mult),
                                 reads=pb.R + ["masks", "hmask"], writes=pb.R)
                        pbs[(bi, kv)] = pb
                for bi, kbk in grp:
                    kt, va = bufs[bi]
                    nk = kbk["nk"]
                    for kv in range(2):
                        pb = pbs[(bi, kv)]
                        for r in range(3):
                            S.op("pe", lambda e: e.matmul(pso[kv][0][:nq, r * 65:(r + 1) * 65], lhsT=pb.ap[:nk, r, 0:nq],
                                                          rhs=va.ap[:nk, kv, 0:65], start=(bi == 0 and r == 0),
                                                          stop=(bi == nkb - 1 and r == 2)),
                                 reads=pb.R + va.R, writes=[pso[kv][1]], inc=(r == 2))
            if AS <= 4:
                return
            for kv in range(2):
                ps, pr = pso[kv]
                v3 = ps[:nq, 0:195].rearrange("p (r c) -> p r c", r=3)
                S.op("act", lambda e: e.activation(
                    out=OSN.ap[:nq, :, :].rearrange("p (r kv) d -> p kv r d", kv=2)[:, kv, :, :], in_=v3[:, :, 0:64],
                    func=AF.Copy), reads=[pr], writes=OSN.R)
                S.op("act", lambda e: e.activation(
                    out=OSD.ap[:nq, :, :].rearrange("p (r kv) d -> p kv r d", kv=2)[:, kv, :, :],
                    in_=v3[:, :, 64:65].broadcast_to([nq, 3, 64]),
                    func=AF.Copy), reads=[pr], writes=OSD.R)
            if AS <= 5:
                return
            for src, dview, is_den in ((OSN, OT, False), (OSD, DT, True)):
                pst, ptr = psum()
                for r in range(3):
                    tin = src.flat[:nq, r * 128:(r + 1) * 128]
                    S.op("pe", lambda e: e.transpose(out=pst[:, r * 128:r * 128 + nq], in_=tin, identity=ident[:nq, :nq]),
                         reads=src.R + ["ident"], writes=[ptr], inc=(r == 2))
                pv3 = pst[:, 0:384].rearrange("p (r q) -> p r q", r=3)[:, :, 0:nq]
                r0 = 0 if is_den else 3 * g
                dst = colsel(dview.ap[:, r0:r0 + 3, :])
                dreg = dview.Rc(r0, r0 + 3)
                if not is_den:
                    S.op("act", lambda e: e.activation(out=dst, in_=pv3, func=AF.Copy), reads=[ptr], writes=dreg)
                elif g == 0:
                    S.op("act", lambda e: e.activation(out=dst, in_=pv3, func=AF.Copy), reads=[ptr], writes=dreg)
                else:
                    S.op("dve", lambda e: e.tensor_tensor(out=dst, in0=pv3, in1=dst, op=ALU.add),
                         reads=[ptr] + dreg, writes=dreg)

        def phase_b_tile(T, stage="all", extra=False):
            sample = (T == 4)
            N = 16 if sample else NT
            if sample:
                x_fn = lambda c: xsres[:, c, :]
                xregs = ["xsres"]
            else:
                x_fn = lambda c: xres[:, c, T * NT:(T + 1) * NT]
                xregs = xr_regs(T)
            x_blk = None if sample else (lambda c0, c1: xres[:, c0:c1, T * NT:(T + 1) * NT])
            mt_blk = None if sample else (lambda c0, c1: MT.ap[:, c0:c1, :])
            ubr = lambda c: ubreg(c, N)
            nb_fn = lambda c: UB.ap[:, c, 15:15 + N]
            mt_fn = lambda c: MT.ap[:, c, :N]
            if stage != "post":
                rms_to(x_fn, xregs, gcol_fn(4), "gains", nb_fn, ubr, N, x_blk)
                wq = w_q.rearrange("(kc p) f -> p kc f", p=128)
                nblk = (N + 127) // 128
                for part in range(3):
                    g = part
                    wb = next_wb()
                    S.dma("sp", wb.ap[:, :, 0:384], wq[:, :, part * 384:(part + 1) * 384], writes=wb.R)
                    for b in range(nblk):
                        nb = min(128, N - b * 128)
                        ps, pr = psum()
                        for kc in range(NCH):
                            S.op("pe", lambda e: e.matmul(ps[:nb, 0:384], lhsT=UB.ap[:, kc, 15 + b * 128:15 + b * 128 + nb],
                                                          rhs=wb.ap[:, kc, 0:384], start=(kc == 0), stop=(kc == NCH - 1)),
                                 reads=ubr(kc) + wb.Rc(kc), writes=[pr], inc=(kc == NCH - 1))
                        blk = 32 if sample else 16 + T * 4 + b
                        slot = (part * nblk + b) % 3
                        qv = QST.flat[:nb, slot * 384:(slot + 1) * 384]
                        qr = QST.Rr(slot * 384, (slot + 1) * 384)
                        rope_evac(ps, pr, nb, blk, qv.rearrange("p (r kv d) -> p kv r d", r=3, kv=2), qr, perm=True)
                        ps2, pr2 = psum()
                        for r in range(3):
                            S.op("pe", lambda e: e.transpose(out=ps2[:, r * 128:r * 128 + nb], in_=qv[:, r * 128:(r + 1) * 128],
                                                             identity=ident[:nb, :nb]),
                                 reads=qr + ["ident"], writes=[pr2], inc=(r == 2))
                        S.op("act", lambda e: e.activation(
                            out=QT.ap[:, 3 * g:3 * g + 3, b * 128:b * 128 + nb],
                            in_=ps2[:, 0:384].rearrange("p (r q) -> p r q", r=3)[:, :, :nb], func=AF.Copy),
                            reads=[pr2], writes=QT.Rc(3 * g, 3 * g + 3))
                if DBG.get('bstop', 99) <= 2:
                    return
                for i in range(4):
                    S.op("dve", lambda e: e.memset(VA[i].ap[:, :, 64:128], 1.0), writes=VA[i].R)
                if not sample:
                    R0 = CH + NT * T
                    kvr = [("kvs", t) for t in range(8)]
                    for g in range(3):
                        kc0, vc0 = g * 128, 384 + g * 128
                        if g == 0:
                            for i in range(4):
                                p0 = R0 - 128 + 128 * i
                                c0 = R0 + 128 * i
                                kbs = [dict(k_src=kvs[p0:p0 + 128, kc0:kc0 + 128], v_src=kvs[p0:p0 + 128, vc0:vc0 + 128], nk=128,
                                            mask=maskA[:, :, :], hm=(hmask[:, 0:1] if (T == 0 and i == 0) else None), reads=kvr),
                                       dict(k_src=kvs[c0:c0 + 128, kc0:kc0 + 128], v_src=kvs[c0:c0 + 128, vc0:vc0 + 128], nk=128,
                                            mask=maskB[:, :, :], hm=None, reads=kvr)]
                                attn_block(0, 128, lambda a, i=i: a[:, :, i * 128:(i + 1) * 128], kbs)
                        elif g == 1:
                            k4 = kvs.rearrange("(a s) c -> a s c", s=4)
                            for rr in range(4):
                                ap0 = (R0 - NT) // 4
                                ac0 = R0 // 4
                                kbs = [dict(k_src=k4[ap0:ap0 + 128, rr, kc0:kc0 + 128], v_src=k4[ap0:ap0 + 128, rr, vc0:vc0 + 128],
                                            nk=128, mask=maskA[:, :, :], hm=(hmask[:, 0:1] if T == 0 else None), reads=kvr),
                                       dict(k_src=k4[ac0:ac0 + 128, rr, kc0:kc0 + 128], v_src=k4[ac0:ac0 + 128, rr, vc0:vc0 + 128],
                                            nk=128, mask=maskB[:, :, :], hm=None, reads=kvr)]
                                attn_block(1, 128, lambda a, rr=rr: a.rearrange("p r (q s) -> p r q s", s=4)[:, :, :, rr], kbs)
                        else:
                            k16 = kvs.rearrange("(a s) c -> a s c", s=16)
                            for rr in range(16):
                                aa0 = (NT * T) // 16
                                ab0 = R0 // 16
                                kbs = [dict(k_src=k16[aa0:aa0 + 128, rr, kc0:kc0 + 128], v_src=k16[aa0:aa0 + 128, rr, vc0:vc0 + 128],
                                            nk=128, mask=maskA[:, :, 0:32], hm=hmask[:, T:T + 1], reads=kvr),
                                       dict(k_src=k16[ab0:ab0 + 32, rr, kc0:kc0 + 128], v_src=k16[ab0:ab0 + 32, rr, vc0:vc0 + 128],
                                            nk=32, mask=maskB[0:32, :, 0:32], hm=None, reads=kvr)]
                                attn_block(2, 32, lambda a, rr=rr: a.rearrange("p r (q s) -> p r q s", s=16)[:, :, :, rr], kbs)
                else:
                    newr = [("kvs", 8)]
                    for g in range(3):
                        kc0, vc0 = g * 128, 384 + g * 128
                        for sq_ in range(4):
                            n0 = 2 * CH + 4 * sq_
                            newb_k = kvs[n0:n0 + 4, kc0:kc0 + 128]
                            newb_v = kvs[n0:n0 + 4, vc0:vc0 + 128]
                            if g == 0:
                                kbs = [dict(k_src=ck[sq_, 1920:2048, 0:128], v_src=cv[sq_, 1920:2048, 0:128], nk=128,
                                            mask=msamp[:, 0], hm=None, reads=[]),
                                       dict(k_src=newb_k, v_src=newb_v, nk=4, mask=msamp[0:4, 5], hm=None, reads=newr)]
                            else:
                                st_ = 4 if g == 1 else 16
                                a0 = 384 if g == 1 else 0
                                ckr = ck[sq_].rearrange("(a t) c -> a t c", t=st_)
                                cvr = cv[sq_].rearrange("(a t) c -> a t c", t=st_)
                                kbs = [dict(k_src=ckr[a0:a0 + 128, i, kc0:kc0 + 128], v_src=cvr[a0:a0 + 128, i, kc0:kc0 + 128],
                                            nk=128, mask=msamp[:, 1 + i], hm=None, reads=[]) for i in range(4)]
                                kbs.append(dict(k_src=newb_k, v_src=newb_v, nk=4, mask=msamp[0:4, 6], hm=None, reads=newr))
                            attn_block(g, 4, lambda a, sq_=sq_: a[:, :, sq_ * 4:(sq_ + 1) * 4], kbs)
                if DBG.get('bstop', 99) <= 4:
                    return
                for r in range(3):
                    S.op("dve", lambda e: e.reciprocal(out=DT.ap[:, r, :N], in_=DT.ap[:, r, :N]), reads=DT.Rc(r), writes=DT.Rc(r))
                for g in range(3):
                    for r in range(3):
                        c = 3 * g + r
                        S.op("pool", lambda e: e.tensor_tensor(out=OT.ap[:, c, :N], in0=OT.ap[:, c, :N], in1=DT.ap[:, r, :N],
                                                               op=ALU.mult), reads=OT.Rc(c) + DT.Rc(r), writes=OT.Rc(c))
                if DBG.get('bstop', 99) <= 5:
                    return
                wo5 = w_o.rearrange("(g kv r d) m -> kv d g r m", g=3, kv=2, r=3, d=64)
                for qtr in range(4):
                    wb = next_wb()
                    wv = wb.flat[:, 0:9 * 256].rearrange("p (c m) -> p c m", c=9)
                    for kv in range(2):
                        for g in range(3):
                            S.dma("sp", wv[kv * 64:(kv + 1) * 64, 3 * g:3 * g + 3, :],
                                  wo5[kv, :, g, :, qtr * 256:(qtr + 1) * 256], writes=wb.Rr(g * 768, (g + 1) * 768))
                    for mm in range(2):
                        ps, pr = psum()
                        for kc in range(9):
                            S.op("pe", lambda e: e.matmul(ps[:, :N], lhsT=wv[:, kc, mm * 128:(mm + 1) * 128], rhs=OT.ap[:, kc, :N],
                                                          start=(kc == 0), stop=(kc == 8)),
                                 reads=wb.R + OT.Rc(kc), writes=[pr], inc=(kc == 8))
                        c = qtr * 2 + mm
                        S.op("act", lambda e: e.activation(out=MT.ap[:, c, :N], in_=ps[:, :N], func=AF.Copy),
                             reads=[pr], writes=MT.Rc(c))
                if DBG.get('bstop', 99) <= 6:
                    return
            if stage != "post":
                rms_residual(mt_fn, MT.R, gcol_fn(5), "gains", x_fn, xregs, N, mt_blk)
            if stage == "pre":
                rms_to(x_fn, xregs, gcol_fn(6), "gains", lambda c: hs[:, c, :], ["hs"], N)
                return
            if stage == "post":
                rms_residual(lambda c: ffs[:, c, :], ["ffs"], gcol_fn(7), "gains", x_fn, xregs, N)
            else:
                rms_to(x_fn, xregs, gcol_fn(6), "gains", nb_fn, ubr, N, x_blk)
                mlp(1, nb_fn, ubr, N, extra=extra)
                rms_residual(mt_fn, MT.R, gcol_fn(7), "gains", x_fn, xregs, N, mt_blk)
            emit_y([16] if sample else list(range(T * 4, T * 4 + 4)))

        if do_phase_b:
            tl_b = list(DBG.get("btiles", list(range(5))))
            merge_b = (3 in tl_b and 4 in tl_b)
            for T in tl_b:
                if merge_b and T == 3:
                    phase_b_tile(4, stage="pre")
                    phase_b_tile(3, extra=True)
                elif merge_b and T == 4:
                    phase_b_tile(4, stage="post")
                else:
                    phase_b_tile(T)
        elif DBG["emit_y"]:
            emit_y(list(range(17)))
        S.finish()
        print("instructions", S.n_inst, "waits", S.n_wait)
    return nc


def _rope_inv():
    try:
        import jax
        import jax.numpy as jnp
        with jax.default_device(jax.devices("cpu")[0]):
            v = 10000.0 ** (-jnp.arange(0, HD, 2, dtype=jnp.float32) / HD)
            return np.asarray(v, dtype=np.float32)
    except Exception:
        e = (-np.arange(0, HD, 2, dtype=np.float32) / np.float32(HD)).astype(np.float32)
        return np.power(np.float32(10000.0), e).astype(np.float32)


def _const_tables(q):
    inv = _rope_inv()
    pos = np.concatenate([(q - 1) * CH + np.arange(2 * CH), PAST + np.tile(np.arange(4), 4)]).astype(np.float32)
    ang = (pos[:, None] * inv[None, :]).astype(np.float32).astype(np.float64)
    cos = np.cos(ang).astype(np.float32)
    sin = np.sin(ang).astype(np.float32)
    rc = np.zeros((33 * 128, 32), np.float32)
    rsn = np.zeros((33 * 128, 32), np.float32)
    rc[:2 * CH + 16] = cos
    rsn[:2 * CH + 16] = sin
    rc = rc.reshape(33, 128, 32).transpose(1, 0, 2).reshape(128, 33 * 32)
    rsn = rsn.reshape(33, 128, 32).transpose(1, 0, 2).reshape(128, 33 * 32)
    return np.ascontiguousarray(rc), np.ascontiguousarray(rsn)


def _icnt(q):
    t = np.zeros((2, 8, 16), np.float32)
    for c in range(8):
        w = 2 << (c // 2)
        t[:, c, :] = 1.0 / w
    posn = np.arange(16)
    for c in range(8):
        w = 2 << (c // 2)
        tab = 1.0 / np.minimum(w, posn + 1).astype(np.float32)
        if q == 1:
            t[0, c, :] = tab
        if q == 0:
            t[1, c, :] = tab
    return np.ascontiguousarray(np.broadcast_to(t.reshape(1, 256), (128, 256))).astype(np.float32)


_PROG = {}


def kernel(x_prompt, x_sample, cache_pool, cache_k, cache_v, norm_gains, kv_norm_gain, w_pool,
           pool_scale, w_q, w_o, w_kv, w_up, w_down, _phase_b=True):
    f32 = np.float32
    x_prompt = np.asarray(x_prompt, f32)
    x_sample = np.asarray(x_sample, f32)
    cache_pool = np.asarray(cache_pool, f32)
    cache_k = np.asarray(cache_k, f32)
    cache_v = np.asarray(cache_v, f32)
    key = bool(_phase_b)
    if key not in _PROG:
        _PROG[key] = build_program(do_phase_b=_phase_b)
    nc = _PROG[key]
    ncores = DBG.get("ncores", 8)

    gl = np.ascontiguousarray(np.asarray(norm_gains, f32).reshape(8, 8, 128).transpose(2, 0, 1).reshape(128, 64))
    kvgl = np.ascontiguousarray(np.asarray(kv_norm_gain, f32).reshape(8, 128).T)
    psl = np.ascontiguousarray(np.asarray(pool_scale, f32).reshape(8, 128).T)
    kk = np.arange(128)[:, None]
    qq = np.arange(128)[None, :]
    mA = np.ascontiguousarray(np.broadcast_to((kk >= qq).astype(f32)[:, None, :], (128, 3, 128))).reshape(128, 384)
    mB = np.ascontiguousarray(np.broadcast_to((kk <= qq).astype(f32)[:, None, :], (128, 3, 128))).reshape(128, 384)
    ms = np.zeros((128, 7, 3, 4), f32)
    i4 = np.arange(4)[None, :]
    ms[:, 0] = (kk >= i4).astype(f32)[:, None, :]
    for i in range(4):
        ms[:, 1 + i, :, i] = 1.0
    ms[:, 5] = (kk <= i4).astype(f32)[:, None, :]
    ms[:, 6] = (kk == i4).astype(f32)[:, None, :]
    ms = ms.reshape(128, 84)
    ident = np.eye(128, dtype=f32)
    shared = {
        "gains": gl, "kvg": kvgl, "pscale": psl, "maskA": mA, "maskB": mB, "msamp": ms, "ident": ident,
        "w_pool": np.ascontiguousarray(np.asarray(w_pool, f32)[0]),
        "w_q": np.ascontiguousarray(np.asarray(w_q, f32)[0]),
        "w_o": np.ascontiguousarray(np.asarray(w_o, f32)[0]),
        "w_kv": np.ascontiguousarray(np.asarray(w_kv, f32)),
        "w_up": np.ascontiguousarray(np.asarray(w_up, f32)),
        "w_down": np.ascontiguousarray(np.asarray(w_down, f32)),
    }
    in_maps = []
    for c in range(8):
        b, q = divmod(c, 4)
        xpc = np.zeros((2 * CH, D), f32)
        xpc[CH:] = x_prompt[b, q * CH:(q + 1) * CH]
        xphc = np.zeros((16, D), f32)
        if q >= 1:
            xpc[:CH] = x_prompt[b, (q - 1) * CH:q * CH]
        if q >= 2:
            xphc[:] = x_prompt[b, (q - 1) * CH - 16:(q - 1) * CH]
        rc, rsn = _const_tables(q)
        hm = np.ones((128, 4), f32)
        if q == 0:
            for T in range(4):
                hm[:, T] = ((32 * T - 128 + np.arange(128)) >= 0).astype(f32)
        m = dict(shared)
        m.update({
            "xp": xpc, "xph": xphc,
            "xs": np.ascontiguousarray(x_sample[4 * c:4 * c + 4].reshape(16, D)),
            "cpool": np.ascontiguousarray(cache_pool[0, 4 * c:4 * c + 4].reshape(60, D)),
            "ck": np.ascontiguousarray(cache_k[4 * c:4 * c + 4].reshape(4, 2048, 384)),
            "cv": np.ascontiguousarray(cache_v[4 * c:4 * c + 4].reshape(4, 2048, 384)),
            "icnt": _icnt(q), "ropec": rc, "ropes": rsn, "hmask": hm,
        })
        in_maps.append(m)

    res = run_bass_kernel_spmd(nc, in_maps[:ncores], core_ids=list(range(ncores)))
    R = res.results
    y_prompt = np.zeros((2, SEQ, D), f32)
    y_sample = np.zeros((32, 4, D), f32)
    pool_prompt = np.zeros((1, 2, 15, D), f32)
    k_prompt = np.zeros((2, 2048, 6, 64), f32)
    v_prompt = np.zeros((2, 2048, 6, 64), f32)
    pool_sample = np.zeros((1, 32, 15, D), f32)
    k_s = np.zeros((32, 4, 6, 64), f32)
    v_s = np.zeros((32, 4, 6, 64), f32)
    for c in range(ncores):
        b, q = divmod(c, 4)
        r = R[c]
        y_prompt[b, q * CH:(q + 1) * CH] = r["y"][:CH]
        y_sample[4 * c:4 * c + 4] = r["y"][CH:CH + 16].reshape(4, 4, D)
        pool_sample[0, 4 * c:4 * c + 4] = r["pools"]
        k_s[4 * c:4 * c + 4] = r["kvo"][CH:CH + 16, :384].reshape(4, 4, 6, 64)
        v_s[4 * c:4 * c + 4] = r["kvo"][CH:CH + 16, 384:].reshape(4, 4, 6, 64)
        if q == 3:
            pool_prompt[0, b] = r["poolp"]
            k_prompt[b] = r["kvo"][:CH, :384].reshape(2048, 6, 64)
            v_prompt[b] = r["kvo"][:CH, 384:].reshape(2048, 6, 64)
    return (y_prompt, y_sample, pool_prompt, k_prompt, v_prompt, pool_sample, k_s, v_s)
```

```python
import math
from contextlib import ExitStack

import numpy as np
import concourse.bass as bass
import concourse.mybir as mybir
from concourse.bass_utils import run_bass_kernel_spmd

F32 = mybir.dt.float32
AF = mybir.ActivationFunctionType
ALU = mybir.AluOpType

D = 1024
NCH = 8
SEQ = 8192
CH = 2048
NT = 512
HD = 64
EPS = 1e-6
PAST = 16384
GRAN = 128
DIL = (1, 4, 16)
DBG = {"tiles": list(range(9)), "pools": True, "emit_y": True}


class Sched:
    ENG = ("pe", "act", "dve", "pool", "sp")

    def __init__(self, nc, stack, n_dma_sems=32):
        self.nc = nc
        self.eng = {"pe": nc.tensor, "act": nc.scalar, "dve": nc.vector, "pool": nc.gpsimd, "sp": nc.sync}
        self.sem = {}
        for e in self.ENG:
            self.sem[e] = stack.enter_context(nc.semaphore("s_" + e))
        self.n_dma = n_dma_sems // 2
        for q in ("sp", "pool"):
            for j in range(self.n_dma):
                self.sem[("d", q, j)] = stack.enter_context(nc.semaphore("s_d%s%d" % (q, j)))
        self.rrq = {"sp": 0, "pool": 0}
        self.cnt = {k: 0 for k in self.sem}
        self.known = {e: {} for e in self.ENG}
        self.last_write = {}
        self.readers = {}
        self.rr = 0
        self.n_wait = 0
        self.n_inst = 0

    def _wait(self, e, tok):
        s, v = tok
        if v <= 0:
            return
        if s == "pe" and e == "pe":
            return
        if self.known[e].get(s, 0) >= v:
            return
        assert v <= self.cnt[s], ("wait on an increment that is not emitted yet", e, tok, self.cnt[s])
        self.eng[e].wait_ge(self.sem[s], v)
        self.known[e][s] = v
        self.n_wait += 1

    def _deps(self, e, reads, writes):
        need = {}
        for r in reads:
            t = self.last_write.get(r)
            if t is not None and need.get(t[0], 0) < t[1]:
                need[t[0]] = t[1]
        for w in writes:
            t = self.last_write.get(w)
            if t is not None and need.get(t[0], 0) < t[1]:
                need[t[0]] = t[1]
            rd = self.readers.get(w)
            if rd:
                for s, v in rd.items():
                    if need.get(s, 0) < v:
                        need[s] = v
        for s, v in need.items():
            self._wait(e, (s, v))

    def _record(self, tok, reads, writes):
        for w in writes:
            self.last_write[w] = tok
            self.readers[w] = {}
        for r in reads:
            d = self.readers.setdefault(r, {})
            if d.get(tok[0], 0) < tok[1]:
                d[tok[0]] = tok[1]

    def op(self, e, fn, reads=(), writes=(), inc=True):
        self._deps(e, reads, writes)
        inst = fn(self.eng[e])
        self.n_inst += 1
        if inc:
            self.cnt[e] += 1
            inst.then_inc(self.sem[e], 1)
            tok = (e, self.cnt[e])
        else:
            tok = (e, self.cnt[e] + 1)
        self._record(tok, reads, writes)
        return tok

    def dma(self, e, out, in_, reads=(), writes=(), **kw):
        j = self.rrq[e]
        self.rrq[e] = (j + 1) % self.n_dma
        key = ("d", e, j)
        self._wait(e, (key, self.cnt[key]))
        self._deps(e, reads, writes)
        inst = self.eng[e].dma_start(out=out, in_=in_, **kw)
        self.n_inst += 1
        self.cnt[key] += 16
        inst.then_inc(self.sem[key], 16)
        tok = (key, self.cnt[key])
        self._record(tok, reads, writes)
        return tok

    def finish(self):
        for e in self.ENG:
            for s in self.sem:
                self._wait(e, (s, self.cnt[s]))


class View:
    def __init__(self, arena, off, d0, d1):
        self.off, self.d0, self.d1 = off, d0, d1
        self.n = d0 * d1
        self.ap = arena[:, off:off + self.n].rearrange("p (a b) -> p a b", a=d0)
        self.flat = arena[:, off:off + self.n]

    def Rr(self, lo, hi):
        a = (self.off + lo) // GRAN
        b = (self.off + hi - 1) // GRAN
        return [("A", g) for g in range(a, b + 1)]

    def Rc(self, c0, c1=None):
        c1 = c0 + 1 if c1 is None else c1
        return self.Rr(c0 * self.d1, c1 * self.d1)

    @property
    def R(self):
        return self.Rr(0, self.n)


def build_program(do_phase_b=True):
    nc = bass.Bass("TRN2", target_bir_lowering=False)

    def din(name, shape):
        return nc.dram_tensor(name, list(shape), F32, kind="ExternalInput").ap()

    def dout(name, shape):
        return nc.dram_tensor(name, list(shape), F32, kind="ExternalOutput").ap()

    xp = din("xp", [2 * CH, D])
    xph = din("xph", [16, D])
    xs = din("xs", [16, D])
    cpool = din("cpool", [60, D])
    ck = din("ck", [4, 2048, 384])
    cv = din("cv", [4, 2048, 384])
    gains_d = din("gains", [128, 64])
    kvg_d = din("kvg", [128, 8])
    pscale_d = din("pscale", [128, 8])
    icnt_d = din("icnt", [128, 256])
    ropec_d = din("ropec", [128, 33 * 32])
    ropes_d = din("ropes", [128, 33 * 32])
    maskA_d = din("maskA", [128, 384])
    maskB_d = din("maskB", [128, 384])
    msamp_d = din("msamp", [128, 7 * 12])
    hmask_d = din("hmask", [128, 4])
    ident_d = din("ident", [128, 128])
    w_pool = din("w_pool", [4, 256, 256])
    w_q = din("w_q", [D, 1152])
    w_o = din("w_o", [1152, D])
    w_kv = din("w_kv", [D, 768])
    w_up = din("w_up", [2, D, 4 * D])
    w_down = din("w_down", [2, 4 * D, D])

    y_o = dout("y", [CH + 16, D])
    kvo = dout("kvo", [CH + 16, 768])
    poolp = dout("poolp", [15, D])
    pools = dout("pools", [4, 15, D])
    kvs = nc.dram_tensor("kvs", [2 * CH + 16, 768], F32, kind="Internal").ap()

    with ExitStack() as st:
        S = Sched(nc, st)

        def sb(name, shape):
            return st.enter_context(nc.sbuf_tensor(name, list(shape), F32))

        xres = sb("xres", [128, NCH, CH])
        xsres = sb("xsres", [128, NCH, 16])
        ident = sb("ident_sb", [128, 128])
        onesm = sb("onesm", [128, 128])
        gains = sb("gains_sb", [128, 64])
        kvg = sb("kvg_sb", [128, 8])
        pscale = sb("pscale_sb", [128, 8])
        icnt = sb("icnt_sb", [128, 2, 8, 16])
        ropec = sb("ropec_sb", [128, 33, 32])
        ropes = sb("ropes_sb", [128, 33, 32])
        maskA = sb("maskA_sb", [128, 3, 128])
        maskB = sb("maskB_sb", [128, 3, 128])
        msamp = sb("msamp_sb", [128, 7, 3, 4])
        hmask = sb("hmask_sb", [128, 4])
        epsb = sb("epsb", [128, 1])
        acc = sb("acc", [128, NT])
        sq = [sb("sq%d" % i, [128, NT]) for i in range(2)]
        rs = [sb("rs%d" % i, [128, NT]) for i in range(2)]
        halo = sb("halo", [128, NCH, 15])
        uexs = sb("uexs", [128, NCH, 76])
        kvst_t = [sb("kvst%d" % i, [128, 768]) for i in range(2)]
        hs = sb("hs", [128, NCH, 16])
        hids = sb("hids", [128, 32, 16])
        ffs = sb("ffs", [128, NCH, 16])
        deferred = []

        def drain(n=None):
            k = len(deferred) if n is None else min(n, len(deferred))
            for _ in range(k):
                deferred.pop(0)()

        preloaded = set()
        ARENA_N = 27136
        arena = sb("arena", [128, ARENA_N])
        psb = [st.enter_context(nc.psum_tensor("psb%d" % i, [128, 512], F32)) for i in range(8)]

        state = {"ps": 0, "rs": 0, "wb": 0, "kb": 0, "pb": 0, "tmp": 0}

        def psum():
            i = state["ps"]
            state["ps"] = (i + 1) % 6
            return psb[i], ("ps", i)

        UB = View(arena, 0, NCH, NT + 15)
        OT = View(arena, 0, 9, NT)
        MT = View(arena, 4608, NCH, NT)
        HID = View(arena, 8704, 16, NT)
        WB = [View(arena, 16896, NCH, NT), View(arena, 20992, NCH, NT)]
        WPOOL = View(arena, 25088, 8, 256)
        DT = View(arena, 25088, 3, NT)
        XB = 27136
        XTOK = [View(arena, 8704 + 1024 * i, 1, 1024) for i in range(2)]
        PA = View(arena, 8704 + 2048, 2, NT + 15)
        PBv = View(arena, 8704 + 3584, 2, NT + 15)
        PL = [View(arena, 8704 + 5120 + 1024 * i, 2, NT) for i in range(2)]
        TMP = [View(arena, 8704 + 7168 + 512 * i, 1, NT) for i in range(2)]
        KVST = [View(arena, 4608 + 1024 * i, 1, 768) for i in range(2)]
        QT = View(arena, 8704, 9, NT)
        QST = View(arena, 8704 + 4608, 1, 1152)
        KBLK = [View(arena, 8704 + 5760 + 128 * i, 1, 128) for i in range(4)]
        KT = [View(arena, 8704 + 6272 + 128 * i, 1, 128) for i in range(4)]
        VA = [View(arena, 8704 + 6784 + 256 * i, 2, 128) for i in range(4)]
        PB = [View(arena, 8704 + 7808, 3, 128)]
        OSN = View(arena, 4608 + 2048, 6, 64)
        OSD = View(arena, 4608 + 2048 + 384, 6, 64)
        for i in range(3):
            PB.append(View(arena, 4608 + 2048 + 768 + 384 * i, 3, 128))
        YTOK = [View(arena, 4608 + 1024 * i, 1, 1024) for i in range(2)]

        def xr_regs(slot):
            return [("xres", slot)]

        for dst, src, nm in ((ident, ident_d, "ident"), (gains, gains_d, "gains"), (kvg, kvg_d, "kvg"),
                             (pscale, pscale_d, "pscale"), (hmask, hmask_d, "hmask")):
            S.dma("pool", dst[:], src, writes=[nm])
        S.dma("pool", icnt[:].rearrange("p a c t -> p (a c t)"), icnt_d, writes=["icnt"])
        S.dma("pool", ropec[:].rearrange("p a b -> p (a b)"), ropec_d, writes=["ropec"])
        S.dma("pool", ropes[:].rearrange("p a b -> p (a b)"), ropes_d, writes=["ropes"])
        S.dma("pool", maskA[:].rearrange("p a b -> p (a b)"), maskA_d, writes=["masks"])
        S.dma("pool", maskB[:].rearrange("p a b -> p (a b)"), maskB_d, writes=["masks"])
        S.dma("pool", msamp[:].rearrange("p a b c -> p (a b c)"), msamp_d, writes=["masks"])
        S.op("dve", lambda e: e.memset(onesm[:], 1.0 / D), writes=["onesm"])
        S.op("dve", lambda e: e.memset(epsb[:], EPS), writes=["epsb"])

        def load_xT(src_rows, ntok, dst_fn, dst_regs):
            nblk = (ntok + 127) // 128
            for b in range(nblk):
                nb = min(128, ntok - b * 128)
                xt = XTOK[b % 2]
                S.dma("sp", xt.flat[:nb, :], src_rows[b * 128:b * 128 + nb, :], writes=xt.R)
                for half in range(2):
                    ps, pr = psum()
                    for cc in range(4):
                        c = half * 4 + cc
                        S.op("pe", lambda e, ps=ps, cc=cc, c=c, xt=xt, nb=nb: e.transpose(
                            out=ps[:, cc * 128:cc * 128 + nb], in_=xt.flat[:nb, c * 128:(c + 1) * 128],
                            identity=ident[:nb, :nb]), reads=xt.R + ["ident"], writes=[pr], inc=(cc == 3))
                    S.op("act", lambda e, ps=ps, half=half, b=b, nb=nb: e.activation(
                        out=dst_fn(half * 4, half * 4 + 4, b * 128, nb),
                        in_=ps[:, :].rearrange("p (a b) -> p a b", a=4)[:, :, :nb], func=AF.Copy),
                        reads=[pr], writes=dst_regs)

        def ubreg(c, N):
            return UB.Rr(c * (NT + 15) + 15, c * (NT + 15) + 15 + N)

        def scr_blk(c0, c1):
            return UB.ap[:, c0:c1, 15:15 + NT]

        def scr_R(c0, c1):
            return UB.Rr(c0 * (NT + 15) + 15, (c1 - 1) * (NT + 15) + 15 + NT)

        def rstd_from(sum_ap, sum_regs, N):
            ps, pr = psum()
            S.op("pe", lambda e: e.matmul(ps[:, :N], lhsT=onesm[:], rhs=sum_ap, start=True, stop=True),
                 reads=sum_regs + ["onesm"], writes=[pr])
            k = state["rs"]
            state["rs"] = 1 - k
            r = rs[k]
            S.op("act", lambda e: e.activation(out=r[:, :N], in_=ps[:, :N], func=AF.Sqrt, bias=epsb[:, 0:1], scale=1.0),
                 reads=[pr, "epsb"], writes=[("rs", k)])
            S.op("dve", lambda e: e.reciprocal(out=r[:, :N], in_=r[:, :N]), reads=[("rs", k)], writes=[("rs", k)])
            return r, ("rs", k)

        def rms_rstd(src_fn, sregs, N, src_blk=None):
            if N == NT and src_blk is not None:
                S.op("act", lambda e: e.activation(out=scr_blk(0, 4), in_=src_blk(0, 4), func=AF.Square),
                     reads=sregs, writes=scr_R(0, 4))
                S.op("dve", lambda e: e.tensor_tensor(out=scr_blk(4, 8), in0=src_blk(4, 8), in1=src_blk(4, 8), op=ALU.mult),
                     reads=sregs, writes=scr_R(4, 8))
                for a, b in ((4, 8), (2, 4), (1, 2)):
                    w = b - a
                    S.op("dve", lambda e, a=a, b=b, w=w: e.tensor_tensor(out=scr_blk(0, w), in0=scr_blk(0, w),
                                                                       in1=scr_blk(a, b), op=ALU.add),
                         reads=scr_R(0, b), writes=scr_R(0, w))
                return rstd_from(UB.ap[:, 0, 15:15 + NT], scr_R(0, 1), N)
            for c in range(NCH):
                if c == 0:
                    S.op("act", lambda e: e.activation(out=acc[:, :N], in_=src_fn(0), func=AF.Square),
                         reads=sregs, writes=["acc"])
                else:
                    sqb = sq[c % 2]
                    S.op("act", lambda e, sqb=sqb, c=c: e.activation(out=sqb[:, :N], in_=src_fn(c), func=AF.Square),
                         reads=sregs, writes=[("sq", c % 2)])
                    S.op("pool", lambda e, sqb=sqb: e.tensor_tensor(out=acc[:, :N], in0=acc[:, :N], in1=sqb[:, :N],
                                                                    op=ALU.add),
                         reads=["acc", ("sq", c % 2)], writes=["acc"])
            return rstd_from(acc[:, :N], ["acc"], N)

        def rms_to(src_fn, sregs, gcol, gname, dst_fn, dregs, N, src_blk=None):
            r, rr = rms_rstd(src_fn, sregs, N, src_blk)
            for c in range(NCH):
                dr = dregs(c) if callable(dregs) else dregs
                S.op("dve", lambda e, c=c: e.scalar_tensor_tensor(
                    out=dst_fn(c), in0=src_fn(c), scalar=gcol(c), in1=r[:, :N], op0=ALU.mult, op1=ALU.mult),
                    reads=sregs + [rr, gname], writes=dr)

        def rms_residual(src_fn, sregs, gcol, gname, x_fn, xregs, N, src_blk=None):
            r, rr = rms_rstd(src_fn, sregs, N, src_blk)
            for c in range(NCH):
                if N == NT and src_blk is not None:
                    tb_ap, tb_r = UB.ap[:, c, 15:15 + NT], ubreg(c, NT)
                else:
                    k = state["tmp"]
                    state["tmp"] = 1 - k
                    tb_ap, tb_r = sq[k][:, :N], [("sq", k)]
                S.op("dve", lambda e, c=c, tb_ap=tb_ap: e.scalar_tensor_tensor(
                    out=tb_ap, in0=src_fn(c), scalar=gcol(c), in1=r[:, :N], op0=ALU.mult, op1=ALU.mult),
                    reads=sregs + [rr, gname], writes=tb_r)
                S.op("pool" if c % 3 != 2 else "dve",
                     lambda e, c=c, tb_ap=tb_ap: e.tensor_tensor(out=x_fn(c), in0=x_fn(c), in1=tb_ap, op=ALU.add),
                     reads=xregs + tb_r, writes=xregs)

        def gcol_fn(n):
            return lambda c: gains[:, n * 8 + c:n * 8 + c + 1]

        def next_wb():
            k = state["wb"]
            state["wb"] = 1 - k
            return WB[k]

        def mlp(layer, h_fn, hregs_fn, N, extra=False):
            wu = w_up[layer].rearrange("(kc p) h -> p kc h", p=128)
            wd = w_down[layer].rearrange("(kc p) f -> p kc f", p=128)
            for half in range(2):
                for jg in range(4):
                    wb = next_wb()
                    h0 = half * 2048 + jg * 512
                    S.dma("sp", wb.ap, wu[:, :, h0:h0 + 512], writes=wb.R)
                    for jj in range(4):
                        j = jg * 4 + jj
                        ps, pr = psum()
                        for kc in range(NCH):
                            S.op("pe", lambda e, ps=ps, wb=wb, kc=kc, jj=jj: e.matmul(
                                ps[:, :N], lhsT=wb.ap[:, kc, jj * 128:(jj + 1) * 128], rhs=h_fn(kc),
                                start=(kc == 0), stop=(kc == NCH - 1)),
                                reads=wb.Rc(kc) + hregs_fn(kc), writes=[pr], inc=(kc == NCH - 1))
                        S.op("act", lambda e, ps=ps, j=j: e.activation(out=HID.ap[:, j, :N], in_=ps[:, :N], func=AF.Relu),
                             reads=[pr], writes=HID.Rc(j))
                        S.op("dve", lambda e, j=j: e.tensor_tensor(out=HID.ap[:, j, :N], in0=HID.ap[:, j, :N],
                                                                    in1=HID.ap[:, j, :N], op=ALU.mult),
                             reads=HID.Rc(j), writes=HID.Rc(j))
                    if extra:
                        ps, pr = psum()
                        for jj in range(4):
                            for kc in range(NCH):
                                S.op("pe", lambda e, ps=ps, wb=wb, kc=kc, jj=jj: e.matmul(
                                    ps[:, jj * 16:(jj + 1) * 16], lhsT=wb.ap[:, kc, jj * 128:(jj + 1) * 128], rhs=hs[:, kc, :],
                                    start=(jj == 0 and kc == 0), stop=(jj == 3 and kc == NCH - 1)),
                                    reads=wb.Rc(kc) + ["hs"], writes=[pr], inc=(jj == 3 and kc == NCH - 1))
                        jb = half * 16 + jg * 4
                        hv = hids[:, jb:jb + 4, :]
                        S.op("act", lambda e, ps=ps, hv=hv: e.activation(
                            out=hv, in_=ps[:, 0:64].rearrange("p (a b) -> p a b", a=4), func=AF.Relu),
                            reads=[pr], writes=["hids"])
                        S.op("dve", lambda e, hv=hv: e.tensor_tensor(out=hv, in0=hv, in1=hv, op=ALU.mult),
                             reads=["hids"], writes=["hids"])
                for mh in range(2):
                    accs = [psum() for _ in range(4)]
                    xacc = psum() if extra else None
                    for kcg in range(4):
                        wb = next_wb()
                        k0 = half * 16 + kcg * 4
                        S.dma("sp", wb.ap[:, 0:4, :], wd[:, k0:k0 + 4, mh * 512:(mh + 1) * 512], writes=wb.Rc(0, 4))
                        for kc in range(4):
                            for m in range(4):
                                ps, pr = accs[m]
                                first = (kcg == 0 and kc == 0)
                                last = (kcg == 3 and kc == 3)
                                S.op("pe", lambda e, ps=ps, wb=wb, kc=kc, m=m, kcg=kcg, first=first, last=last: e.matmul(
                                    ps[:, :N], lhsT=wb.ap[:, kc, m * 128:(m + 1) * 128], rhs=HID.ap[:, kcg * 4 + kc, :N],
                                    start=first, stop=last),
                                    reads=wb.Rc(kc) + HID.Rc(kcg * 4 + kc), writes=[pr],
                                    inc=(last or (kc == 3 and m == 3 and not extra)))
                        if extra:
                            ps, pr = xacc
                            for kc in range(4):
                                for m in range(4):
                                    first = (kcg == 0 and kc == 0 and m == 0)
                                    last = (kcg == 3 and kc == 3 and m == 3)
                                    S.op("pe", lambda e, ps=ps, wb=wb, kc=kc, m=m, first=first, last=last: e.matmul(
                                        ps[:, m * 16:(m + 1) * 16], lhsT=wb.ap[:, kc, m * 128:(m + 1) * 128],
                                        rhs=hids[:, k0 + kc, :], start=first, stop=last),
                                        reads=wb.Rc(kc) + ["hids"], writes=[pr], inc=(kc == 3 and m == 3))
                    for m in range(4):
                        ps, pr = accs[m]
                        c = mh * 4 + m
                        if half == 0:
                            S.op("act", lambda e, ps=ps, c=c: e.activation(out=MT.ap[:, c, :N], in_=ps[:, :N], func=AF.Copy),
                                 reads=[pr], writes=MT.Rc(c))
                        else:
                            S.op("dve", lambda e, ps=ps, c=c: e.tensor_tensor(out=MT.ap[:, c, :N], in0=ps[:, :N],
                                                                             in1=MT.ap[:, c, :N], op=ALU.add),
                                 reads=[pr] + MT.Rc(c), writes=MT.Rc(c))
                    if extra:
                        ps, pr = xacc
                        fv = ffs[:, mh * 4:mh * 4 + 4, :]
                        pv = ps[:, 0:64].rearrange("p (a b) -> p a b", a=4)
                        if half == 0:
                            S.op("act", lambda e: e.activation(out=fv, in_=pv, func=AF.Copy), reads=[pr], writes=["ffs"])
                        else:
                            S.op("dve", lambda e: e.tensor_tensor(out=fv, in0=pv, in1=fv, op=ALU.add),
                                 reads=[pr, "ffs"], writes=["ffs"])

        def rope_evac(ps, pr, nb, blk, dst, dregs, perm=False):
            k = state["tmp"]
            state["tmp"] = 1 - k
            tb = sq[k]
            if perm:
                pv = ps[:nb, 0:384].rearrange("p (kv r d) -> p kv r d", kv=2, r=3)
                cb = ropec[:nb, blk, :].unsqueeze(1).unsqueeze(1).broadcast_to([nb, 2, 3, 32])
                sbb = ropes[:nb, blk, :].unsqueeze(1).unsqueeze(1).broadcast_to([nb, 2, 3, 32])
                tv = tb[:nb, 0:192].rearrange("p (kv r d) -> p kv r d", kv=2, r=3)
                lo = lambda a: a[:, :, :, 0:32]
                hi = lambda a: a[:, :, :, 32:64]
            else:
                pv = ps[:nb, 0:384].rearrange("p (h d) -> p h d", h=6)
                cb = ropec[:nb, blk, :].unsqueeze(1).broadcast_to([nb, 6, 32])
                sbb = ropes[:nb, blk, :].unsqueeze(1).broadcast_to([nb, 6, 32])
                tv = tb[:nb, 0:192].rearrange("p (h d) -> p h d", h=6)
                lo = lambda a: a[:, :, 0:32]
                hi = lambda a: a[:, :, 32:64]
            rd = [pr, "ropec", "ropes"]
            S.op("dve", lambda e: e.tensor_tensor(out=lo(dst), in0=lo(pv), in1=cb, op=ALU.mult), reads=rd, writes=dregs)
            S.op("dve", lambda e: e.tensor_tensor(out=tv, in0=hi(pv), in1=sbb, op=ALU.mult), reads=rd, writes=[("sq", k)])
            S.op("dve", lambda e: e.tensor_tensor(out=lo(dst), in0=lo(dst), in1=tv, op=ALU.subtract),
                 reads=dregs + [("sq", k)], writes=dregs)
            S.op("dve", lambda e: e.tensor_tensor(out=hi(dst), in0=hi(pv), in1=cb, op=ALU.mult), reads=rd, writes=dregs)
            S.op("dve", lambda e: e.tensor_tensor(out=tv, in0=lo(pv), in1=sbb, op=ALU.mult),
                 reads=rd + [("sq", k)], writes=[("sq", k)])
            S.op("dve", lambda e: e.tensor_tensor(out=hi(dst), in0=hi(dst), in1=tv, op=ALU.add),
                 reads=dregs + [("sq", k)], writes=dregs)

        def phase_a_tile(ti, stage="all", extra=False):
            sample = (ti == 8)
            N = 16 if sample else NT
            if sample:
                x_fn = lambda c: xsres[:, c, :]
                xregs = ["xsres"]
                x_dst = lambda c0, c1, t0, nb: xsres[:, c0:c1, t0:t0 + nb]
                src_rows = xs
                x_blk = None
                mt_blk = None
            else:
                slot = ti % 4
                x_fn = lambda c: xres[:, c, slot * NT:(slot + 1) * NT]
                xregs = xr_regs(slot)
                x_dst = lambda c0, c1, t0, nb: xres[:, c0:c1, slot * NT + t0:slot * NT + t0 + nb]
                src_rows = xp[ti * NT:(ti + 1) * NT, :]
                x_blk = lambda c0, c1: xres[:, c0:c1, slot * NT:(slot + 1) * NT]
                mt_blk = lambda c0, c1: MT.ap[:, c0:c1, :]
            ubr = lambda c: ubreg(c, N)
            nb_fn = lambda c: UB.ap[:, c, 15:15 + N]
            if stage != "post":
                if ti == 0:
                    load_xT(xph, 16, lambda c0, c1, t0, nb: uexs[:, c0:c1, 0:16], ["uexs"])
                    rms_to(lambda c: uexs[:, c, 0:16], ["uexs"], gcol_fn(0), "gains",
                           lambda c: uexs[:, c, 16:32], ["uexs"], 16)
                    S.op("pool", lambda e: e.tensor_copy(out=UB.ap[:, :, 0:15], in_=uexs[:, :, 17:32]),
                         reads=["uexs"], writes=UB.R)
                elif not sample:
                    S.op("pool", lambda e: e.tensor_copy(out=UB.ap[:, :, 0:15], in_=halo[:]), reads=["halo"], writes=UB.R)
                drain(1)
                if ti not in preloaded:
                    load_xT(src_rows, N, x_dst, xregs)
                    preloaded.add(ti)
                if DBG.get('stop', 99) <= 1:
                    return
                if sample:
                    load_xT(cpool, 60, lambda c0, c1, t0, nb: uexs[:, c0:c1, :].rearrange(
                        "p c (s t) -> p c s t", s=4)[:, :, :, 0:15], ["uexs"])
                    rms_to(x_fn, xregs, gcol_fn(0), "gains", nb_fn, UB.R, N)
                    S.op("pool", lambda e: e.tensor_copy(
                        out=uexs[:, :, :].rearrange("p c (s t) -> p c s t", s=4)[:, :, :, 15:19],
                        in_=UB.ap[:, :, 15:31].rearrange("p c (s t) -> p c s t", s=4)), reads=UB.R, writes=["uexs"])
                    uext = lambda c0, c1: uexs[:, c0:c1, :]
                    uregs = ["uexs"]
                    E = 76
                else:
                    rms_to(x_fn, xregs, gcol_fn(0), "gains", nb_fn, ubr, N, x_blk)
                    drain(1)
                    S.op("pool", lambda e: e.tensor_copy(out=halo[:], in_=UB.ap[:, :, NT:NT + 15]), reads=UB.R, writes=["halo"])
                    uext = lambda c0, c1: UB.ap[:, c0:c1, :]
                    uregs = UB.R
                    E = NT + 15
                if DBG.get('stop', 99) <= 2:
                    return
                if ti == 7 or sample:
                    drain()
                if ti == 7:
                    for half in range(2):
                        ps, pr = psum()
                        for cc in range(4):
                            c = half * 4 + cc
                            S.op("pe", lambda e, ps=ps, cc=cc, c=c: e.transpose(
                                out=ps[:15, cc * 128:(cc + 1) * 128], in_=UB.ap[:, c, NT:NT + 15], identity=ident[:, :]),
                                reads=UB.R + ["ident"], writes=[pr], inc=(cc == 3))
                        S.op("act", lambda e, ps=ps, half=half: e.activation(
                            out=YTOK[0].flat[:15, half * 512:(half + 1) * 512], in_=ps[:15, :], func=AF.Copy),
                            reads=[pr], writes=YTOK[0].R)
                    S.dma("pool", poolp, YTOK[0].flat[:15, :], reads=YTOK[0].R, writes=["poolp"])
                if sample:
                    for half in range(2):
                        ps, pr = psum()
                        for cc in range(4):
                            c = half * 4 + cc
                            S.op("pe", lambda e, ps=ps, cc=cc, c=c: e.transpose(
                                out=ps[:16, cc * 128:(cc + 1) * 128],
                                in_=UB.ap[:, c, 15:31], identity=ident[:, :]),
                                reads=UB.R + ["ident"], writes=[pr], inc=(cc == 3))
                        S.op("act", lambda e, ps=ps, half=half: e.activation(
                            out=YTOK[0].flat[:16, half * 512:(half + 1) * 512], in_=ps[:16, :], func=AF.Copy),
                            reads=[pr], writes=YTOK[0].R)
                    for s in range(4):
                        S.dma("pool", pools[s, 0:11, :], cpool[s * 15 + 4:s * 15 + 15, :], writes=[("pools", s)])
                        S.dma("pool", pools[s, 11:15, :], YTOK[0].flat[s * 4:(s + 1) * 4, :], reads=YTOK[0].R,
                              writes=[("pools", s)])
                if DBG.get('stop', 99) <= 3:
                    return
                for g in range(4):
                    w = 2 << g
                    src = uext(2 * g, 2 * g + 2)
                    sreg = uregs
                    bufs = [PA, PBv]
                    for k in range(g + 1):
                        sh = 1 << k
                        lo = (2 << k) - 1
                        dv = bufs[k % 2]
                        dst = dv.ap[:, :, 0:E]
                        S.op("dve", lambda e, dst=dst, src=src, lo=lo, sh=sh: e.tensor_tensor(
                            out=dst[:, :, lo:E], in0=src[:, :, lo:E], in1=src[:, :, lo - sh:E - sh], op=ALU.add),
                            reads=sreg, writes=dv.R)
                        src, sreg = dst, dv.R
                    pl = PL[g % 2]
                    u2 = uext(2 * g, 2 * g + 2)
                    if sample:
                        sv = src.rearrange("p c (s t) -> p c s t", s=4)
                        uv = u2.rearrange("p c (s t) -> p c s t", s=4)
                        for cc in range(2):
                            S.op("dve", lambda e, cc=cc: e.scalar_tensor_tensor(
                                out=pl.ap[:, cc, 0:16].rearrange("p (s t) -> p s t", s=4), in0=sv[:, cc, :, 15:19],
                                scalar=1.0 / w, in1=uv[:, cc, :, 15:19], op0=ALU.mult, op1=ALU.subtract),
                                reads=sreg + uregs, writes=pl.R)
                    else:
                        S.op("dve", lambda e: e.scalar_tensor_tensor(
                            out=pl.ap[:, :, :], in0=src[:, :, 15:E], scalar=1.0 / w, in1=u2[:, :, 15:E],
                            op0=ALU.mult, op1=ALU.subtract), reads=sreg + uregs, writes=pl.R)
                        if ti in (0, 4):
                            wh = 0 if ti == 0 else 1
                            S.op("dve", lambda e: e.tensor_tensor(out=pl.ap[:, :, 0:16], in0=src[:, :, 15:31],
                                                                  in1=icnt[:, wh, 2 * g:2 * g + 2, :], op=ALU.mult),
                                 reads=sreg + ["icnt"], writes=pl.R)
                            S.op("dve", lambda e: e.tensor_tensor(out=pl.ap[:, :, 0:16], in0=pl.ap[:, :, 0:16],
                                                                  in1=u2[:, :, 15:31], op=ALU.subtract),
                                 reads=pl.R + uregs, writes=pl.R)
                    drain()
                    for eo in range(2):
                        ps, pr = psum()
                        for cc in range(2):
                            S.op("pe", lambda e, ps=ps, cc=cc, eo=eo: e.matmul(
                                ps[:, :N], lhsT=WPOOL.ap[:, g * 2 + cc, eo * 128:(eo + 1) * 128], rhs=pl.ap[:, cc, :N],
                                start=(cc == 0), stop=(cc == 1)),
                                reads=WPOOL.R + pl.R, writes=[pr], inc=(cc == 1))
                        c = 2 * g + eo
                        S.op("act", lambda e, ps=ps, c=c: e.activation(out=MT.ap[:, c, :N], in_=ps[:, :N], func=AF.Copy,
                                                                        scale=pscale[:, c:c + 1]),
                             reads=[pr, "pscale"], writes=MT.Rc(c))
                if DBG.get('stop', 99) <= 4:
                    return
                mt_fn = lambda c: MT.ap[:, c, :N]
                tl = DBG["tiles"]
                nxt = tl[tl.index(ti) + 1] if tl.index(ti) + 1 < len(tl) else None
                if nxt is not None and nxt not in preloaded:
                    if nxt == 8:
                        load_xT(xs, 16, lambda c0, c1, t0, nb: xsres[:, c0:c1, t0:t0 + nb], ["xsres"])
                    else:
                        ns = nxt % 4
                        load_xT(xp[nxt * NT:(nxt + 1) * NT, :], NT,
                                lambda c0, c1, t0, nb, ns=ns: xres[:, c0:c1, ns * NT + t0:ns * NT + t0 + nb], xr_regs(ns))
                    preloaded.add(nxt)
                rms_residual(mt_fn, MT.R, gcol_fn(1), "gains", x_fn, xregs, N, mt_blk)
                if DBG.get('stop', 99) <= 5:
                    return
            mt_fn = lambda c: MT.ap[:, c, :N]
            if stage == "pre":
                rms_to(x_fn, xregs, gcol_fn(2), "gains", lambda c: hs[:, c, :], ["hs"], N)
                return
            if stage == "post":
                drain()
                rms_residual(lambda c: ffs[:, c, :], ["ffs"], gcol_fn(3), "gains", x_fn, xregs, N)
            else:
                rms_to(x_fn, xregs, gcol_fn(2), "gains", nb_fn, ubr, N, x_blk)
                if DBG.get('stop', 99) <= 6:
                    return
                mlp(0, nb_fn, ubr, N, extra=extra)
                if DBG.get('stop', 99) <= 7:
                    return
                rms_residual(mt_fn, MT.R, gcol_fn(3), "gains", x_fn, xregs, N, mt_blk)
            if DBG.get('stop', 99) <= 8:
                return
            rms_to(x_fn, xregs, lambda c: kvg[:, c:c + 1], "kvg", mt_fn, lambda c: MT.Rc(c), N, x_blk)
            wkv = w_kv.rearrange("(kc p) f -> p kc f", p=128)
            wbs = []
            for part in range(2):
                wb = next_wb()
                S.dma("sp", wb.ap[:, :, 0:384], wkv[:, :, part * 384:(part + 1) * 384], writes=wb.R)
                wbs.append(wb)
            nblk = (N + 127) // 128

            def kv_block(b):
                nb = min(128, N - b * 128)
                kst = kvst_t[b % 2]
                kreg = [("kvst", b % 2)]
                pss = []
                for part in range(2):
                    ps, pr = psum()
                    wb = wbs[part]
                    for kc in range(NCH):
                        S.op("pe", lambda e, ps=ps, wb=wb, kc=kc: e.matmul(
                            ps[:nb, 0:384], lhsT=MT.ap[:, kc, b * 128:b * 128 + nb], rhs=wb.ap[:, kc, 0:384],
                            start=(kc == 0), stop=(kc == NCH - 1)),
                            reads=MT.Rc(kc) + wb.Rc(kc), writes=[pr], inc=(kc == NCH - 1))
                    pss.append((ps, pr))
                blk = 32 if sample else ti * 4 + b
                rope_evac(pss[0][0], pss[0][1], nb, blk, kst[:nb, 0:384].rearrange("p (h d) -> p h d", h=6), kreg)
                S.op("act", lambda e: e.activation(out=kst[:nb, 384:768], in_=pss[1][0][:nb, 0:384], func=AF.Copy),
                     reads=[pss[1][1]], writes=kreg)
                if sample:
                    S.dma("pool", kvs[2 * CH:2 * CH + 16, :], kst[:16, :], reads=kreg, writes=[("kvs", 8)])
                    S.dma("pool", kvo[CH:CH + 16, :], kst[:16, :], reads=kreg, writes=["kvo"])
                else:
                    r0 = ti * NT + b * 128
                    S.dma("pool", kvs[r0:r0 + 128, :], kst[:, :], reads=kreg, writes=[("kvs", ti)])
                    if ti >= 4:
                        S.dma("pool", kvo[r0 - CH:r0 - CH + 128, :], kst[:, :], reads=kreg, writes=["kvo"])

            for b in range(nblk):
                deferred.append(lambda b=b: kv_block(b))

        for g in range(4):
            S.dma("sp", WPOOL.ap[:, g * 2:g * 2 + 2, :], w_pool[g].rearrange("(cc p) e -> p cc e", p=128),
                  writes=WPOOL.Rc(g * 2, g * 2 + 2))
        tl_a = list(DBG["tiles"])
        merge_a = (7 in tl_a and 8 in tl_a)
        for ti in tl_a:
            if merge_a and ti == 7:
                phase_a_tile(8, stage="pre")
                phase_a_tile(7, extra=True)
            elif merge_a and ti == 8:
                phase_a_tile(8, stage="post")
            else:
                phase_a_tile(ti)
        drain()

        def emit_y(blocks):
            for b in blocks:
                nb = 128 if b < 16 else 16
                yt = YTOK[b % 2]
                for half in range(2):
                    ps, pr = psum()
                    for cc in range(4):
                        c = half * 4 + cc
                        src = xres[:, c, b * 128:(b + 1) * 128] if b < 16 else xsres[:, c, :]
                        rd = [("xres", b // 4)] if b < 16 else ["xsres"]
                        S.op("pe", lambda e, ps=ps, cc=cc, src=src, nb=nb: e.transpose(
                            out=ps[:nb, cc * 128:(cc + 1) * 128], in_=src, identity=ident[:, :]),
                            reads=rd + ["ident"], writes=[pr], inc=(cc == 3))
                    S.op("act", lambda e, ps=ps, half=half, nb=nb, yt=yt: e.activation(
                        out=yt.flat[:nb, half * 512:(half + 1) * 512], in_=ps[:nb, :], func=AF.Copy),
                        reads=[pr], writes=yt.R)
                S.dma("pool", y_o[b * 128:b * 128 + nb, :], yt.flat[:nb, :], reads=yt.R, writes=["y"])

        pso = [(psb[6], ("ps", 6)), (psb[7], ("ps", 7))]

        def attn_block(g, nq, colsel, kblocks):
            if DBG.get('nonew') and kblocks[-1]['nk'] == 4:
                kblocks = kblocks[:-1]
            if g not in DBG.get('agroups', (0, 1, 2)):
                return
            nkb = len(kblocks)
            AS = DBG.get('astop', 99)
            for p0 in range(0, nkb, 2):
                grp = list(enumerate(kblocks))[p0:p0 + 2]
                bufs = {}
                for bi, kbk in grp:
                    i = state["kb"]
                    state["kb"] = (i + 1) % 4
                    kb, kt, va = KBLK[i], KT[i], VA[i]
                    bufs[bi] = (kt, va)
                    nk = kbk["nk"]
                    S.dma("pool", kb.flat[:nk, :], kbk["k_src"], reads=kbk["reads"], writes=kb.R)
                    S.dma("pool", va.ap[:nk, :, 0:64], kbk["v_src"].rearrange("n (h d) -> n h d", h=2),
                          reads=kbk["reads"], writes=va.R)
                    pst, ptr = psum()
                    S.op("pe", lambda e: e.transpose(out=pst[:, :nk], in_=kb.flat[:nk, :], identity=ident[:nk, :nk]),
                         reads=kb.R + ["ident"], writes=[ptr])
                    S.op("act", lambda e: e.activation(out=kt.flat[:, :nk], in_=pst[:, :nk], func=AF.Copy),
                         reads=[ptr], writes=kt.R)
                svs = {}
                for bi, kbk in grp:
                    kt, va = bufs[bi]
                    nk = kbk["nk"]
                    for kv in range(2):
                        pss, psr = psum()
                        sv = pss[:nk, 0:3 * nq].rearrange("p (r q) -> p r q", r=3)
                        S.op("pe", lambda e: e.matmul(sv, lhsT=kt.flat[kv * 64:(kv + 1) * 64, :nk],
                                                      rhs=colsel(QT.ap[kv * 64:(kv + 1) * 64, 3 * g:3 * g + 3, :]),
                                                      start=True, stop=True),
                             reads=kt.R + QT.Rc(3 * g, 3 * g + 3), writes=[psr])
                        svs[(bi, kv)] = (sv, psr)
                pbs = {}
                for bi, kbk in grp:
                    nk = kbk["nk"]
                    for kv in range(2):
                        sv, psr = svs[(bi, kv)]
                        j = state["pb"]
                        state["pb"] = (j + 1) % 4
                        pb = PB[j]
                        pv = pb.ap[:nk, :, 0:nq]
                        S.op("act", lambda e: e.activation(out=pv, in_=sv, func=AF.Exp, scale=0.125),
                             reads=[psr], writes=pb.R)
                        mk = kbk["mask"]
                        if kbk["hm"] is None:
                            S.op("dve", lambda e: e.tensor_tensor(out=pv, in0=pv, in1=mk, op=ALU.mult),
                                 reads=pb.R + ["masks"], writes=pb.R)
                        else:
                            S.op("dve", lambda e: e.scalar_tensor_tensor(out=pv, in0=pv, scalar=kbk["hm"], in1=mk,
                                                                          op0=ALU.mult, op1=ALU.mult),
                                 reads=pb.R + ["masks", "hmask"], writes=pb.R)
                        pbs[(bi, kv)] = pb
                for bi, kbk in grp:
                    kt, va = bufs[bi]
                    nk = kbk["nk"]
                    for kv in range(2):
                        pb = pbs[(bi, kv)]
                        for r in range(3):
                            S.op("pe", lambda e: e.matmul(pso[kv][0][:nq, r * 65:(r + 1) * 65], lhsT=pb.ap[:nk, r, 0:nq],
                                                          rhs=va.ap[:nk, kv, 0:65], start=(bi == 0 and r == 0),
                                                          stop=(bi == nkb - 1 and r == 2)),
                                 reads=pb.R + va.R, writes=[pso[kv][1]], inc=(r == 2))
            if AS <= 4:
                return
            for kv in range(2):
                ps, pr = pso[kv]
                v3 = ps[:nq, 0:195].rearrange("p (r c) -> p r c", r=3)
                S.op("act", lambda e: e.activation(
                    out=OSN.ap[:nq, kv * 3:(kv + 1) * 3, :], in_=v3[:, :, 0:64],
                    func=AF.Copy), reads=[pr], writes=OSN.R)
                S.op("act", lambda e: e.activation(
                    out=OSD.ap[:nq, kv * 3:(kv + 1) * 3, :],
                    in_=v3[:, :, 64:65].broadcast_to([nq, 3, 64]),
                    func=AF.Copy), reads=[pr], writes=OSD.R)
            if AS <= 5:
                return
            for src, dview, is_den in ((OSN, OT, False), (OSD, DT, True)):
                pst, ptr = psum()
                for r in range(3):
                    tin = src.flat[:nq, r * 128:(r + 1) * 128]
                    S.op("pe", lambda e: e.transpose(out=pst[:, r * 128:r * 128 + nq], in_=tin, identity=ident[:nq, :nq]),
                         reads=src.R + ["ident"], writes=[ptr], inc=(r == 2))
                pv3 = pst[:, 0:384].rearrange("p (r q) -> p r q", r=3)[:, :, 0:nq]
                r0 = 0 if is_den else 3 * g
                dst = colsel(dview.ap[:, r0:r0 + 3, :])
                dreg = dview.Rc(r0, r0 + 3)
                if not is_den:
                    S.op("act", lambda e: e.activation(out=dst, in_=pv3, func=AF.Copy), reads=[ptr], writes=dreg)
                elif g == 0:
                    S.op("act", lambda e: e.activation(out=dst, in_=pv3, func=AF.Copy), reads=[ptr], writes=dreg)
                else:
                    S.op("dve", lambda e: e.tensor_tensor(out=dst, in0=pv3, in1=dst, op=ALU.add),
                         reads=[ptr] + dreg, writes=dreg)

        def phase_b_tile(T, stage="all", extra=False):
            sample = (T == 4)
            N = 16 if sample else NT
            if sample:
                x_fn = lambda c: xsres[:, c, :]
                xregs = ["xsres"]
            else:
                x_fn = lambda c: xres[:, c, T * NT:(T + 1) * NT]
                xregs = xr_regs(T)
            x_blk = None if sample else (lambda c0, c1: xres[:, c0:c1, T * NT:(T + 1) * NT])
            mt_blk = None if sample else (lambda c0, c1: MT.ap[:, c0:c1, :])
            ubr = lambda c: ubreg(c, N)
            nb_fn = lambda c: UB.ap[:, c, 15:15 + N]
            mt_fn = lambda c: MT.ap[:, c, :N]
            if stage != "post":
                rms_to(x_fn, xregs, gcol_fn(4), "gains", nb_fn, ubr, N, x_blk)
                wq = w_q.rearrange("(kc p) f -> p kc f", p=128)
                nblk = (N + 127) // 128
                pend = []
                for part in range(3):
                    g = part
                    wb = next_wb()
                    S.dma("sp", wb.ap[:, :, 0:384], wq[:, :, part * 384:(part + 1) * 384], writes=wb.R)
                    for b in range(nblk):
                        nb = min(128, N - b * 128)
                        ps, pr = psum()
                        for kc in range(NCH):
                            S.op("pe", lambda e: e.matmul(ps[:nb, 0:384], lhsT=UB.ap[:, kc, 15 + b * 128:15 + b * 128 + nb],
                                                          rhs=wb.ap[:, kc, 0:384], start=(kc == 0), stop=(kc == NCH - 1)),
                                 reads=ubr(kc) + wb.Rc(kc), writes=[pr], inc=(kc == NCH - 1))
                        while pend:
                            pend.pop(0)()
                        blk = 32 if sample else 16 + T * 4 + b
                        slot = (part * nblk + b) % 3
                        qv = QST.flat[:nb, slot * 384:(slot + 1) * 384]
                        qr = QST.Rr(slot * 384, (slot + 1) * 384)
                        rope_evac(ps, pr, nb, blk, qv.rearrange("p (r kv d) -> p kv r d", r=3, kv=2), qr, perm=True)

                        def q_transposes(g=g, b=b, nb=nb, qv=qv, qr=qr):
                            ps2, pr2 = psum()
                            for r in range(3):
                                S.op("pe", lambda e: e.transpose(out=ps2[:, r * 128:r * 128 + nb],
                                                                 in_=qv[:, r * 128:(r + 1) * 128], identity=ident[:nb, :nb]),
                                     reads=qr + ["ident"], writes=[pr2], inc=(r == 2))
                            S.op("act", lambda e: e.activation(
                                out=QT.ap[:, 3 * g:3 * g + 3, b * 128:b * 128 + nb],
                                in_=ps2[:, 0:384].rearrange("p (r q) -> p r q", r=3)[:, :, :nb], func=AF.Copy),
                                reads=[pr2], writes=QT.Rc(3 * g, 3 * g + 3))
                        pend.append(q_transposes)
                while pend:
                    pend.pop(0)()
                if DBG.get('bstop', 99) <= 2:
                    return
                for i in range(4):
                    S.op("dve", lambda e: e.memset(VA[i].ap[:, :, 64:128], 1.0), writes=VA[i].R)
                if not sample:
                    R0 = CH + NT * T
                    kvr = [("kvs", t) for t in range(8)]
                    for g in range(3):
                        kc0, vc0 = g * 128, 384 + g * 128
                        if g == 0:
                            for i in range(4):
                                p0 = R0 - 128 + 128 * i
                                c0 = R0 + 128 * i
                                kbs = [dict(k_src=kvs[p0:p0 + 128, kc0:kc0 + 128], v_src=kvs[p0:p0 + 128, vc0:vc0 + 128], nk=128,
                                            mask=maskA[:, :, :], hm=(hmask[:, 0:1] if (T == 0 and i == 0) else None), reads=kvr),
                                       dict(k_src=kvs[c0:c0 + 128, kc0:kc0 + 128], v_src=kvs[c0:c0 + 128, vc0:vc0 + 128], nk=128,
                                            mask=maskB[:, :, :], hm=None, reads=kvr)]
                                attn_block(0, 128, lambda a, i=i: a[:, :, i * 128:(i + 1) * 128], kbs)
                        elif g == 1:
                            k4 = kvs.rearrange("(a s) c -> a s c", s=4)
                            for rr in range(4):
                                ap0 = (R0 - NT) // 4
                                ac0 = R0 // 4
                                kbs = [dict(k_src=k4[ap0:ap0 + 128, rr, kc0:kc0 + 128], v_src=k4[ap0:ap0 + 128, rr, vc0:vc0 + 128],
                                            nk=128, mask=maskA[:, :, :], hm=(hmask[:, 0:1] if T == 0 else None), reads=kvr),
                                       dict(k_src=k4[ac0:ac0 + 128, rr, kc0:kc0 + 128], v_src=k4[ac0:ac0 + 128, rr, vc0:vc0 + 128],
                                            nk=128, mask=maskB[:, :, :], hm=None, reads=kvr)]
                                attn_block(1, 128, lambda a, rr=rr: a.rearrange("p r (q s) -> p r q s", s=4)[:, :, :, rr], kbs)
                        else:
                            k16 = kvs.rearrange("(a s) c -> a s c", s=16)
                            for rr in range(16):
                                aa0 = (NT * T) // 16
                                ab0 = R0 // 16
                                kbs = [dict(k_src=k16[aa0:aa0 + 128, rr, kc0:kc0 + 128], v_src=k16[aa0:aa0 + 128, rr, vc0:vc0 + 128],
                                            nk=128, mask=maskA[:, :, 0:32], hm=hmask[:, T:T + 1], reads=kvr),
                                       dict(k_src=k16[ab0:ab0 + 32, rr, kc0:kc0 + 128], v_src=k16[ab0:ab0 + 32, rr, vc0:vc0 + 128],
                                            nk=32, mask=maskB[0:32, :, 0:32], hm=None, reads=kvr)]
                                attn_block(2, 32, lambda a, rr=rr: a.rearrange("p r (q s) -> p r q s", s=16)[:, :, :, rr], kbs)
                else:
                    newr = [("kvs", 8)]
                    for g in range(3):
                        kc0, vc0 = g * 128, 384 + g * 128
                        for sq_ in range(4):
                            n0 = 2 * CH + 4 * sq_
                            newb_k = kvs[n0:n0 + 4, kc0:kc0 + 128]
                            newb_v = kvs[n0:n0 + 4, vc0:vc0 + 128]
                            if g == 0:
                                kbs = [dict(k_src=ck[sq_, 1920:2048, 0:128], v_src=cv[sq_, 1920:2048, 0:128], nk=128,
                                            mask=msamp[:, 0], hm=None, reads=[]),
                                       dict(k_src=newb_k, v_src=newb_v, nk=4, mask=msamp[0:4, 5], hm=None, reads=newr)]
                            else:
                                st_ = 4 if g == 1 else 16
                                a0 = 384 if g == 1 else 0
                                ckr = ck[sq_].rearrange("(a t) c -> a t c", t=st_)
                                cvr = cv[sq_].rearrange("(a t) c -> a t c", t=st_)
                                kbs = [dict(k_src=ckr[a0:a0 + 128, i, kc0:kc0 + 128], v_src=cvr[a0:a0 + 128, i, kc0:kc0 + 128],
                                            nk=128, mask=msamp[:, 1 + i], hm=None, reads=[]) for i in range(4)]
                                kbs.append(dict(k_src=newb_k, v_src=newb_v, nk=4, mask=msamp[0:4, 6], hm=None, reads=newr))
                            attn_block(g, 4, lambda a, sq_=sq_: a[:, :, sq_ * 4:(sq_ + 1) * 4], kbs)
                if DBG.get('bstop', 99) <= 4:
                    return
                for r in range(3):
                    S.op("dve", lambda e: e.reciprocal(out=DT.ap[:, r, :N], in_=DT.ap[:, r, :N]), reads=DT.Rc(r), writes=DT.Rc(r))
                for g in range(3):
                    for r in range(3):
                        c = 3 * g + r
                        S.op("pool", lambda e: e.tensor_tensor(out=OT.ap[:, c, :N], in0=OT.ap[:, c, :N], in1=DT.ap[:, r, :N],
                                                               op=ALU.mult), reads=OT.Rc(c) + DT.Rc(r), writes=OT.Rc(c))
                if DBG.get('bstop', 99) <= 5:
                    return
                wo3 = w_o.rearrange("(c p) m -> p c m", p=128)
                for qtr in range(4):
                    wb = next_wb()
                    wv = wb.flat[:, 0:9 * 256].rearrange("p (c m) -> p c m", c=9)
                    S.dma("sp", wv, wo3[:, :, qtr * 256:(qtr + 1) * 256], writes=wb.R)
                    for mm in range(2):
                        ps, pr = psum()
                        for kc in range(9):
                            S.op("pe", lambda e: e.matmul(ps[:, :N], lhsT=wv[:, kc, mm * 128:(mm + 1) * 128], rhs=OT.ap[:, kc, :N],
                                                          start=(kc == 0), stop=(kc == 8)),
                                 reads=wb.R + OT.Rc(kc), writes=[pr], inc=(kc == 8))
                        c = qtr * 2 + mm
                        S.op("act", lambda e: e.activation(out=MT.ap[:, c, :N], in_=ps[:, :N], func=AF.Copy),
                             reads=[pr], writes=MT.Rc(c))
                if DBG.get('bstop', 99) <= 6:
                    return
            if stage != "post":
                rms_residual(mt_fn, MT.R, gcol_fn(5), "gains", x_fn, xregs, N, mt_blk)
            if stage == "pre":
                rms_to(x_fn, xregs, gcol_fn(6), "gains", lambda c: hs[:, c, :], ["hs"], N)
                return
            if stage == "post":
                rms_residual(lambda c: ffs[:, c, :], ["ffs"], gcol_fn(7), "gains", x_fn, xregs, N)
            else:
                rms_to(x_fn, xregs, gcol_fn(6), "gains", nb_fn, ubr, N, x_blk)
                mlp(1, nb_fn, ubr, N, extra=extra)
                rms_residual(mt_fn, MT.R, gcol_fn(7), "gains", x_fn, xregs, N, mt_blk)
            emit_y([16] if sample else list(range(T * 4, T * 4 + 4)))

        if do_phase_b:
            tl_b = list(DBG.get("btiles", list(range(5))))
            merge_b = (3 in tl_b and 4 in tl_b)
            for T in tl_b:
                if merge_b and T == 3:
                    phase_b_tile(4, stage="pre")
                    phase_b_tile(3, extra=True)
                elif merge_b and T == 4:
                    phase_b_tile(4, stage="post")
                else:
                    phase_b_tile(T)
        elif DBG["emit_y"]:
            emit_y(list(range(17)))
        S.finish()
        print("instructions", S.n_inst, "waits", S.n_wait)
    return nc


def _rope_inv():
    try:
        import jax
        import jax.numpy as jnp
        with jax.default_device(jax.devices("cpu")[0]):
            v = 10000.0 ** (-jnp.arange(0, HD, 2, dtype=jnp.float32) / HD)
            return np.asarray(v, dtype=np.float32)
    except Exception:
        e = (-np.arange(0, HD, 2, dtype=np.float32) / np.float32(HD)).astype(np.float32)
        return np.power(np.float32(10000.0), e).astype(np.float32)


def _const_tables(q):
    inv = _rope_inv()
    pos = np.concatenate([(q - 1) * CH + np.arange(2 * CH), PAST + np.tile(np.arange(4), 4)]).astype(np.float32)
    ang = (pos[:, None] * inv[None, :]).astype(np.float32).astype(np.float64)
    cos = np.cos(ang).astype(np.float32)
    sin = np.sin(ang).astype(np.float32)
    rc = np.zeros((33 * 128, 32), np.float32)
    rsn = np.zeros((33 * 128, 32), np.float32)
    rc[:2 * CH + 16] = cos
    rsn[:2 * CH + 16] = sin
    rc = rc.reshape(33, 128, 32).transpose(1, 0, 2).reshape(128, 33 * 32)
    rsn = rsn.reshape(33, 128, 32).transpose(1, 0, 2).reshape(128, 33 * 32)
    return np.ascontiguousarray(rc), np.ascontiguousarray(rsn)


def _icnt(q):
    t = np.zeros((2, 8, 16), np.float32)
    for c in range(8):
        w = 2 << (c // 2)
        t[:, c, :] = 1.0 / w
    posn = np.arange(16)
    for c in range(8):
        w = 2 << (c // 2)
        tab = 1.0 / np.minimum(w, posn + 1).astype(np.float32)
        if q == 1:
            t[0, c, :] = tab
        if q == 0:
            t[1, c, :] = tab
    return np.ascontiguousarray(np.broadcast_to(t.reshape(1, 256), (128, 256))).astype(np.float32)


_PROG = {}


def kernel(x_prompt, x_sample, cache_pool, cache_k, cache_v, norm_gains, kv_norm_gain, w_pool,
           pool_scale, w_q, w_o, w_kv, w_up, w_down, _phase_b=True):
    f32 = np.float32
    x_prompt = np.asarray(x_prompt, f32)
    x_sample = np.asarray(x_sample, f32)
    cache_pool = np.asarray(cache_pool, f32)
    cache_k = np.asarray(cache_k, f32)
    cache_v = np.asarray(cache_v, f32)
    key = bool(_phase_b)
    if key not in _PROG:
        _PROG[key] = build_program(do_phase_b=_phase_b)
    nc = _PROG[key]
    ncores = DBG.get("ncores", 8)

    gl = np.ascontiguousarray(np.asarray(norm_gains, f32).reshape(8, 8, 128).transpose(2, 0, 1).reshape(128, 64))
    kvgl = np.ascontiguousarray(np.asarray(kv_norm_gain, f32).reshape(8, 128).T)
    psl = np.ascontiguousarray(np.asarray(pool_scale, f32).reshape(8, 128).T)
    kk = np.arange(128)[:, None]
    qq = np.arange(128)[None, :]
    mA = np.ascontiguousarray(np.broadcast_to((kk >= qq).astype(f32)[:, None, :], (128, 3, 128))).reshape(128, 384)
    mB = np.ascontiguousarray(np.broadcast_to((kk <= qq).astype(f32)[:, None, :], (128, 3, 128))).reshape(128, 384)
    ms = np.zeros((128, 7, 3, 4), f32)
    i4 = np.arange(4)[None, :]
    ms[:, 0] = (kk >= i4).astype(f32)[:, None, :]
    for i in range(4):
        ms[:, 1 + i, :, i] = 1.0
    ms[:, 5] = (kk <= i4).astype(f32)[:, None, :]
    ms[:, 6] = (kk == i4).astype(f32)[:, None, :]
    ms = ms.reshape(128, 84)
    ident = np.eye(128, dtype=f32)
    shared = {
        "gains": gl, "kvg": kvgl, "pscale": psl, "maskA": mA, "maskB": mB, "msamp": ms, "ident": ident,
        "w_pool": np.ascontiguousarray(np.asarray(w_pool, f32)[0]),
        "w_q": np.ascontiguousarray(np.asarray(w_q, f32)[0]),
        "w_o": np.ascontiguousarray(np.asarray(w_o, f32)[0]),
        "w_kv": np.ascontiguousarray(np.asarray(w_kv, f32)),
        "w_up": np.ascontiguousarray(np.asarray(w_up, f32)),
        "w_down": np.ascontiguousarray(np.asarray(w_down, f32)),
    }
    in_maps = []
    for c in range(8):
        b, q = divmod(c, 4)
        xpc = np.zeros((2 * CH, D), f32)
        xpc[CH:] = x_prompt[b, q * CH:(q + 1) * CH]
        xphc = np.zeros((16, D), f32)
        if q >= 1:
            xpc[:CH] = x_prompt[b, (q - 1) * CH:q * CH]
        if q >= 2:
            xphc[:] = x_prompt[b, (q - 1) * CH - 16:(q - 1) * CH]
        rc, rsn = _const_tables(q)
        hm = np.ones((128, 4), f32)
        if q == 0:
            for T in range(4):
                hm[:, T] = ((32 * T - 128 + np.arange(128)) >= 0).astype(f32)
        m = dict(shared)
        m.update({
            "xp": xpc, "xph": xphc,
            "xs": np.ascontiguousarray(x_sample[4 * c:4 * c + 4].reshape(16, D)),
            "cpool": np.ascontiguousarray(cache_pool[0, 4 * c:4 * c + 4].reshape(60, D)),
            "ck": np.ascontiguousarray(cache_k[4 * c:4 * c + 4].reshape(4, 2048, 384)),
            "cv": np.ascontiguousarray(cache_v[4 * c:4 * c + 4].reshape(4, 2048, 384)),
            "icnt": _icnt(q), "ropec": rc, "ropes": rsn, "hmask": hm,
        })
        in_maps.append(m)

    res = run_bass_kernel_spmd(nc, in_maps[:ncores], core_ids=list(range(ncores)))
    R = res.results
    y_prompt = np.zeros((2, SEQ, D), f32)
    y_sample = np.zeros((32, 4, D), f32)
    pool_prompt = np.zeros((1, 2, 15, D), f32)
    k_prompt = np.zeros((2, 2048, 6, 64), f32)
    v_prompt = np.zeros((2, 2048, 6, 64), f32)
    pool_sample = np.zeros((1, 32, 15, D), f32)
    k_s = np.zeros((32, 4, 6, 64), f32)
    v_s = np.zeros((32, 4, 6, 64), f32)
    for c in range(ncores):
        b, q = divmod(c, 4)
        r = R[c]
        y_prompt[b, q * CH:(q + 1) * CH] = r["y"][:CH]
        y_sample[4 * c:4 * c + 4] = r["y"][CH:CH + 16].reshape(4, 4, D)
        pool_sample[0, 4 * c:4 * c + 4] = r["pools"]
        k_s[4 * c:4 * c + 4] = r["kvo"][CH:CH + 16, :384].reshape(4, 4, 6, 64)
        v_s[4 * c:4 * c + 4] = r["kvo"][CH:CH + 16, 384:].reshape(4, 4, 6, 64)
        if q == 3:
            pool_prompt[0, b] = r["poolp"]
            k_prompt[b] = r["kvo"][:CH, :384].reshape(2048, 6, 64)
            v_prompt[b] = r["kvo"][:CH, 384:].reshape(2048, 6, 64)
    return (y_prompt, y_sample, pool_prompt, k_prompt, v_prompt, pool_sample, k_s, v_s)
```

```python
import math
from contextlib import ExitStack

import numpy as np
import concourse.bass as bass
import concourse.mybir as mybir
from concourse.bass_utils import run_bass_kernel_spmd

F32 = mybir.dt.float32
AF = mybir.ActivationFunctionType
ALU = mybir.AluOpType

D = 1024
NCH = 8
SEQ = 8192
CH = 2048
NT = 512
HD = 64
EPS = 1e-6
PAST = 16384
GRAN = 128
DIL = (1, 4, 16)
DBG = {"tiles": list(range(9)), "pools": True, "emit_y": True}


class Sched:
    ENG = ("pe", "act", "dve", "pool", "sp")

    def __init__(self, nc, stack, n_dma_sems=32):
        self.nc = nc
        self.eng = {"pe": nc.tensor, "act": nc.scalar, "dve": nc.vector, "pool": nc.gpsimd, "sp": nc.sync}
        self.sem = {}
        for e in self.ENG:
            self.sem[e] = stack.enter_context(nc.semaphore("s_" + e))
        self.n_dma = n_dma_sems // 2
        for q in ("sp", "pool"):
            for j in range(self.n_dma):
                self.sem[("d", q, j)] = stack.enter_context(nc.semaphore("s_d%s%d" % (q, j)))
        self.rrq = {"sp": 0, "pool": 0}
        self.cnt = {k: 0 for k in self.sem}
        self.known = {e: {} for e in self.ENG}
        self.last_write = {}
        self.readers = {}
        self.rr = 0
        self.n_wait = 0
        self.n_inst = 0

    def _wait(self, e, tok):
        s, v = tok
        if v <= 0:
            return
        if s == "pe" and e == "pe":
            return
        if self.known[e].get(s, 0) >= v:
            return
        assert v <= self.cnt[s], ("wait on an increment that is not emitted yet", e, tok, self.cnt[s])
        self.eng[e].wait_ge(self.sem[s], v)
        self.known[e][s] = v
        self.n_wait += 1

    def _deps(self, e, reads, writes):
        need = {}
        for r in reads:
            t = self.last_write.get(r)
            if t is not None and need.get(t[0], 0) < t[1]:
                need[t[0]] = t[1]
        for w in writes:
            t = self.last_write.get(w)
            if t is not None and need.get(t[0], 0) < t[1]:
                need[t[0]] = t[1]
            rd = self.readers.get(w)
            if rd:
                for s, v in rd.items():
                    if need.get(s, 0) < v:
                        need[s] = v
        for s, v in need.items():
            self._wait(e, (s, v))

    def _record(self, tok, reads, writes):
        for w in writes:
            self.last_write[w] = tok
            self.readers[w] = {}
        for r in reads:
            d = self.readers.setdefault(r, {})
            if d.get(tok[0], 0) < tok[1]:
                d[tok[0]] = tok[1]

    def op(self, e, fn, reads=(), writes=(), inc=True):
        self._deps(e, reads, writes)
        inst = fn(self.eng[e])
        self.n_inst += 1
        if inc:
            self.cnt[e] += 1
            inst.then_inc(self.sem[e], 1)
            tok = (e, self.cnt[e])
        else:
            tok = (e, self.cnt[e] + 1)
        self._record(tok, reads, writes)
        return tok

    def dma(self, e, out, in_, reads=(), writes=(), **kw):
        j = self.rrq[e]
        self.rrq[e] = (j + 1) % self.n_dma
        key = ("d", e, j)
        self._wait(e, (key, self.cnt[key]))
        self._deps(e, reads, writes)
        inst = self.eng[e].dma_start(out=out, in_=in_, **kw)
        self.n_inst += 1
        self.cnt[key] += 16
        inst.then_inc(self.sem[key], 16)
        tok = (key, self.cnt[key])
        self._record(tok, reads, writes)
        return tok

    def finish(self):
        for e in self.ENG:
            for s in self.sem:
                self._wait(e, (s, self.cnt[s]))


class View:
    def __init__(self, arena, off, d0, d1):
        self.off, self.d0, self.d1 = off, d0, d1
        self.n = d0 * d1
        self.ap = arena[:, off:off + self.n].rearrange("p (a b) -> p a b", a=d0)
        self.flat = arena[:, off:off + self.n]

    def Rr(self, lo, hi):
        a = (self.off + lo) // GRAN
        b = (self.off + hi - 1) // GRAN
        return [("A", g) for g in range(a, b + 1)]

    def Rc(self, c0, c1=None):
        c1 = c0 + 1 if c1 is None else c1
        return self.Rr(c0 * self.d1, c1 * self.d1)

    @property
    def R(self):
        return self.Rr(0, self.n)


def build_program(do_phase_b=True):
    nc = bass.Bass("TRN2", target_bir_lowering=False)

    def din(name, shape):
        return nc.dram_tensor(name, list(shape), F32, kind="ExternalInput").ap()

    def dout(name, shape):
        return nc.dram_tensor(name, list(shape), F32, kind="ExternalOutput").ap()

    xp = din("xp", [2 * CH, D])
    xph = din("xph", [16, D])
    xs = din("xs", [16, D])
    cpool = din("cpool", [60, D])
    ck = din("ck", [4, 2048, 384])
    cv = din("cv", [4, 2048, 384])
    gains_d = din("gains", [128, 64])
    kvg_d = din("kvg", [128, 8])
    pscale_d = din("pscale", [128, 8])
    icnt_d = din("icnt", [128, 256])
    ropec_d = din("ropec", [128, 33 * 32])
    ropes_d = din("ropes", [128, 33 * 32])
    maskA_d = din("maskA", [128, 384])
    maskB_d = din("maskB", [128, 384])
    msamp_d = din("msamp", [128, 7 * 12])
    hmask_d = din("hmask", [128, 4])
    ident_d = din("ident", [128, 128])
    w_pool = din("w_pool", [4, 256, 256])
    w_q = din("w_q", [D, 1152])
    w_o = din("w_o", [1152, D])
    w_kv = din("w_kv", [D, 768])
    w_up = din("w_up", [2, D, 4 * D])
    w_down = din("w_down", [2, 4 * D, D])

    y_o = dout("y", [CH + 16, D])
    kvo = dout("kvo", [CH + 16, 768])
    poolp = dout("poolp", [15, D])
    pools = dout("pools", [4, 15, D])
    kvs = nc.dram_tensor("kvs", [2 * CH + 16, 768], F32, kind="Internal").ap()

    with ExitStack() as st:
        S = Sched(nc, st)

        def sb(name, shape):
            return st.enter_context(nc.sbuf_tensor(name, list(shape), F32))

        xres = sb("xres", [128, NCH, CH])
        xsres = sb("xsres", [128, NCH, 16])
        ident = sb("ident_sb", [128, 128])
        onesm = sb("onesm", [128, 128])
        gains = sb("gains_sb", [128, 64])
        kvg = sb("kvg_sb", [128, 8])
        pscale = sb("pscale_sb", [128, 8])
        icnt = sb("icnt_sb", [128, 2, 8, 16])
        ropec = sb("ropec_sb", [128, 33, 32])
        ropes = sb("ropes_sb", [128, 33, 32])
        maskA = sb("maskA_sb", [128, 3, 128])
        maskB = sb("maskB_sb", [128, 3, 128])
        msamp = sb("msamp_sb", [128, 7, 3, 4])
        hmask = sb("hmask_sb", [128, 4])
        epsb = sb("epsb", [128, 1])
        acc = sb("acc", [128, NT])
        sq = [sb("sq%d" % i, [128, NT]) for i in range(2)]
        rs = [sb("rs%d" % i, [128, NT]) for i in range(2)]
        halo = sb("halo", [128, NCH, 15])
        uexs = sb("uexs", [128, NCH, 76])
        kvst_t = [sb("kvst%d" % i, [128, 768]) for i in range(2)]
        hs = sb("hs", [128, NCH, 16])
        hids = sb("hids", [128, 32, 16])
        ffs = sb("ffs", [128, NCH, 16])
        deferred = []

        def drain(n=None):
            k = len(deferred) if n is None else min(n, len(deferred))
            for _ in range(k):
                deferred.pop(0)()

        preloaded = set()
        ARENA_N = 27136
        arena = sb("arena", [128, ARENA_N])
        psb = [st.enter_context(nc.psum_tensor("psb%d" % i, [128, 512], F32)) for i in range(8)]

        state = {"ps": 0, "rs": 0, "wb": 0, "kb": 0, "pb": 0, "tmp": 0}

        def psum():
            i = state["ps"]
            state["ps"] = (i + 1) % 6
            return psb[i], ("ps", i)

        UB = View(arena, 0, NCH, NT + 15)
        OT = View(arena, 0, 9, NT)
        MT = View(arena, 4608, NCH, NT)
        HID = View(arena, 8704, 16, NT)
        WB = [View(arena, 16896, NCH, NT), View(arena, 20992, NCH, NT)]
        WPOOL = View(arena, 25088, 8, 256)
        DT = View(arena, 25088, 3, NT)
        XB = 27136
        XTOK = [View(arena, 8704 + 1024 * i, 1, 1024) for i in range(2)]
        PA = View(arena, 8704 + 2048, 2, NT + 15)
        PBv = View(arena, 8704 + 3584, 2, NT + 15)
        PL = [View(arena, 8704 + 5120 + 1024 * i, 2, NT) for i in range(2)]
        TMP = [View(arena, 8704 + 7168 + 512 * i, 1, NT) for i in range(2)]
        KVST = [View(arena, 4608 + 1024 * i, 1, 768) for i in range(2)]
        QT = View(arena, 8704, 9, NT)
        QST = View(arena, 8704 + 4608, 1, 1152)
        KBLK = [View(arena, 8704 + 5760 + 128 * i, 1, 128) for i in range(4)]
        KT = [View(arena, 8704 + 6272 + 128 * i, 1, 128) for i in range(4)]
        VA = [View(arena, 8704 + 6784 + 256 * i, 2, 128) for i in range(4)]
        PB = [View(arena, 8704 + 7808, 3, 128)]
        OSN = View(arena, 4608 + 2048, 6, 64)
        OSD = View(arena, 4608 + 2048 + 384, 6, 64)
        for i in range(3):
            PB.append(View(arena, 4608 + 2048 + 768 + 384 * i, 3, 128))
        YTOK = [View(arena, 4608 + 1024 * i, 1, 1024) for i in range(2)]

        def xr_regs(slot):
            return [("xres", slot)]

        for dst, src, nm in ((ident, ident_d, "ident"), (gains, gains_d, "gains"), (kvg, kvg_d, "kvg"),
                             (pscale, pscale_d, "pscale"), (hmask, hmask_d, "hmask")):
            S.dma("pool", dst[:], src, writes=[nm])
        S.dma("pool", icnt[:].rearrange("p a c t -> p (a c t)"), icnt_d, writes=["icnt"])
        S.dma("pool", ropec[:].rearrange("p a b -> p (a b)"), ropec_d, writes=["ropec"])
        S.dma("pool", ropes[:].rearrange("p a b -> p (a b)"), ropes_d, writes=["ropes"])
        S.dma("pool", maskA[:].rearrange("p a b -> p (a b)"), maskA_d, writes=["masks"])
        S.dma("pool", maskB[:].rearrange("p a b -> p (a b)"), maskB_d, writes=["masks"])
        S.dma("pool", msamp[:].rearrange("p a b c -> p (a b c)"), msamp_d, writes=["masks"])
        S.op("dve", lambda e: e.memset(onesm[:], 1.0 / D), writes=["onesm"])
        S.op("dve", lambda e: e.memset(epsb[:], EPS), writes=["epsb"])

        def load_xT(src_rows, ntok, dst_fn, dst_regs):
            nblk = (ntok + 127) // 128
            for b in range(nblk):
                nb = min(128, ntok - b * 128)
                xt = XTOK[b % 2]
                S.dma("sp", xt.flat[:nb, :], src_rows[b * 128:b * 128 + nb, :], writes=xt.R)
                for half in range(2):
                    ps, pr = psum()
                    for cc in range(4):
                        c = half * 4 + cc
                        S.op("pe", lambda e, ps=ps, cc=cc, c=c, xt=xt, nb=nb: e.transpose(
                            out=ps[:, cc * 128:cc * 128 + nb], in_=xt.flat[:nb, c * 128:(c + 1) * 128],
                            identity=ident[:nb, :nb]), reads=xt.R + ["ident"], writes=[pr], inc=(cc == 3))
                    S.op("act", lambda e, ps=ps, half=half, b=b, nb=nb: e.activation(
                        out=dst_fn(half * 4, half * 4 + 4, b * 128, nb),
                        in_=ps[:, :].rearrange("p (a b) -> p a b", a=4)[:, :, :nb], func=AF.Copy),
                        reads=[pr], writes=dst_regs)

        def ubreg(c, N):
            return UB.Rr(c * (NT + 15) + 15, c * (NT + 15) + 15 + N)

        def scr_blk(c0, c1):
            return UB.ap[:, c0:c1, 15:15 + NT]

        def scr_R(c0, c1):
            return UB.Rr(c0 * (NT + 15) + 15, (c1 - 1) * (NT + 15) + 15 + NT)

        def rstd_from(sum_ap, sum_regs, N):
            ps, pr = psum()
            S.op("pe", lambda e: e.matmul(ps[:, :N], lhsT=onesm[:], rhs=sum_ap, start=True, stop=True),
                 reads=sum_regs + ["onesm"], writes=[pr])
            k = state["rs"]
            state["rs"] = 1 - k
            r = rs[k]
            S.op("act", lambda e: e.activation(out=r[:, :N], in_=ps[:, :N], func=AF.Sqrt, bias=epsb[:, 0:1], scale=1.0),
                 reads=[pr, "epsb"], writes=[("rs", k)])
            S.op("dve", lambda e: e.reciprocal(out=r[:, :N], in_=r[:, :N]), reads=[("rs", k)], writes=[("rs", k)])
            return r, ("rs", k)

        def rms_rstd(src_fn, sregs, N, src_blk=None):
            if N == NT and src_blk is not None:
                S.op("act", lambda e: e.activation(out=scr_blk(0, 4), in_=src_blk(0, 4), func=AF.Square),
                     reads=sregs, writes=scr_R(0, 4))
                S.op("dve", lambda e: e.tensor_tensor(out=scr_blk(4, 8), in0=src_blk(4, 8), in1=src_blk(4, 8), op=ALU.mult),
                     reads=sregs, writes=scr_R(4, 8))
                for a, b in ((4, 8), (2, 4), (1, 2)):
                    w = b - a
                    S.op("dve", lambda e, a=a, b=b, w=w: e.tensor_tensor(out=scr_blk(0, w), in0=scr_blk(0, w),
                                                                       in1=scr_blk(a, b), op=ALU.add),
                         reads=scr_R(0, b), writes=scr_R(0, w))
                return rstd_from(UB.ap[:, 0, 15:15 + NT], scr_R(0, 1), N)
            for c in range(NCH):
                if c == 0:
                    S.op("act", lambda e: e.activation(out=acc[:, :N], in_=src_fn(0), func=AF.Square),
                         reads=sregs, writes=["acc"])
                else:
                    sqb = sq[c % 2]
                    S.op("act", lambda e, sqb=sqb, c=c: e.activation(out=sqb[:, :N], in_=src_fn(c), func=AF.Square),
                         reads=sregs, writes=[("sq", c % 2)])
                    S.op("pool", lambda e, sqb=sqb: e.tensor_tensor(out=acc[:, :N], in0=acc[:, :N], in1=sqb[:, :N],
                                                                    op=ALU.add),
                         reads=["acc", ("sq", c % 2)], writes=["acc"])
            return rstd_from(acc[:, :N], ["acc"], N)

        def rms_to(src_fn, sregs, gcol, gname, dst_fn, dregs, N, src_blk=None):
            r, rr = rms_rstd(src_fn, sregs, N, src_blk)
            for c in range(NCH):
                dr = dregs(c) if callable(dregs) else dregs
                S.op("dve", lambda e, c=c: e.scalar_tensor_tensor(
                    out=dst_fn(c), in0=src_fn(c), scalar=gcol(c), in1=r[:, :N], op0=ALU.mult, op1=ALU.mult),
                    reads=sregs + [rr, gname], writes=dr)

        def rms_residual(src_fn, sregs, gcol, gname, x_fn, xregs, N, src_blk=None):
            r, rr = rms_rstd(src_fn, sregs, N, src_blk)
            for c in range(NCH):
                if N == NT and src_blk is not None:
                    tb_ap, tb_r = UB.ap[:, c, 15:15 + NT], ubreg(c, NT)
                else:
                    k = state["tmp"]
                    state["tmp"] = 1 - k
                    tb_ap, tb_r = sq[k][:, :N], [("sq", k)]
                S.op("dve", lambda e, c=c, tb_ap=tb_ap: e.scalar_tensor_tensor(
                    out=tb_ap, in0=src_fn(c), scalar=gcol(c), in1=r[:, :N], op0=ALU.mult, op1=ALU.mult),
                    reads=sregs + [rr, gname], writes=tb_r)
                S.op("pool" if c % 3 != 2 else "dve",
                     lambda e, c=c, tb_ap=tb_ap: e.tensor_tensor(out=x_fn(c), in0=x_fn(c), in1=tb_ap, op=ALU.add),
                     reads=xregs + tb_r, writes=xregs)

        def gcol_fn(n):
            return lambda c: gains[:, n * 8 + c:n * 8 + c + 1]

        def next_wb():
            k = state["wb"]
            state["wb"] = 1 - k
            return WB[k]

        def mlp(layer, h_fn, hregs_fn, N, extra=False):
            wu = w_up[layer].rearrange("(kc p) h -> p kc h", p=128)
            wd = w_down[layer].rearrange("(kc p) f -> p kc f", p=128)
            for half in range(2):
                for jg in range(4):
                    wb = next_wb()
                    h0 = half * 2048 + jg * 512
                    S.dma("sp", wb.ap, wu[:, :, h0:h0 + 512], writes=wb.R)
                    for jj in range(4):
                        j = jg * 4 + jj
                        ps, pr = psum()
                        for kc in range(NCH):
                            S.op("pe", lambda e, ps=ps, wb=wb, kc=kc, jj=jj: e.matmul(
                                ps[:, :N], lhsT=wb.ap[:, kc, jj * 128:(jj + 1) * 128], rhs=h_fn(kc),
                                start=(kc == 0), stop=(kc == NCH - 1)),
                                reads=wb.Rc(kc) + hregs_fn(kc), writes=[pr], inc=(kc == NCH - 1))
                        S.op("act", lambda e, ps=ps, j=j: e.activation(out=HID.ap[:, j, :N], in_=ps[:, :N], func=AF.Relu),
                             reads=[pr], writes=HID.Rc(j))
                        S.op("dve", lambda e, j=j: e.tensor_tensor(out=HID.ap[:, j, :N], in0=HID.ap[:, j, :N],
                                                                    in1=HID.ap[:, j, :N], op=ALU.mult),
                             reads=HID.Rc(j), writes=HID.Rc(j))
                    if extra:
                        ps, pr = psum()
                        for jj in range(4):
                            for kc in range(NCH):
                                S.op("pe", lambda e, ps=ps, wb=wb, kc=kc, jj=jj: e.matmul(
                                    ps[:, jj * 16:(jj + 1) * 16], lhsT=wb.ap[:, kc, jj * 128:(jj + 1) * 128], rhs=hs[:, kc, :],
                                    start=(jj == 0 and kc == 0), stop=(jj == 3 and kc == NCH - 1)),
                                    reads=wb.Rc(kc) + ["hs"], writes=[pr], inc=(jj == 3 and kc == NCH - 1))
                        jb = half * 16 + jg * 4
                        hv = hids[:, jb:jb + 4, :]
                        S.op("act", lambda e, ps=ps, hv=hv: e.activation(
                            out=hv, in_=ps[:, 0:64].rearrange("p (a b) -> p a b", a=4), func=AF.Relu),
                            reads=[pr], writes=["hids"])
                        S.op("dve", lambda e, hv=hv: e.tensor_tensor(out=hv, in0=hv, in1=hv, op=ALU.mult),
                             reads=["hids"], writes=["hids"])
                for mh in range(2):
                    accs = [psum() for _ in range(4)]
                    xacc = psum() if extra else None
                    for kcg in range(4):
                        wb = next_wb()
                        k0 = half * 16 + kcg * 4
                        S.dma("sp", wb.ap[:, 0:4, :], wd[:, k0:k0 + 4, mh * 512:(mh + 1) * 512], writes=wb.Rc(0, 4))
                        for kc in range(4):
                            for m in range(4):
                                ps, pr = accs[m]
                                first = (kcg == 0 and kc == 0)
                                last = (kcg == 3 and kc == 3)
                                S.op("pe", lambda e, ps=ps, wb=wb, kc=kc, m=m, kcg=kcg, first=first, last=last: e.matmul(
                                    ps[:, :N], lhsT=wb.ap[:, kc, m * 128:(m + 1) * 128], rhs=HID.ap[:, kcg * 4 + kc, :N],
                                    start=first, stop=last),
                                    reads=wb.Rc(kc) + HID.Rc(kcg * 4 + kc), writes=[pr],
                                    inc=(last or (kc == 3 and m == 3 and not extra)))
                        if extra:
                            ps, pr = xacc
                            for kc in range(4):
                                for m in range(4):
                                    first = (kcg == 0 and kc == 0 and m == 0)
                                    last = (kcg == 3 and kc == 3 and m == 3)
                                    S.op("pe", lambda e, ps=ps, wb=wb, kc=kc, m=m, first=first, last=last: e.matmul(
                                        ps[:, m * 16:(m + 1) * 16], lhsT=wb.ap[:, kc, m * 128:(m + 1) * 128],
                                        rhs=hids[:, k0 + kc, :], start=first, stop=last),
                                        reads=wb.Rc(kc) + ["hids"], writes=[pr], inc=(kc == 3 and m == 3))
                    for m in range(4):
                        ps, pr = accs[m]
                        c = mh * 4 + m
                        if half == 0:
                            S.op("act", lambda e, ps=ps, c=c: e.activation(out=MT.ap[:, c, :N], in_=ps[:, :N], func=AF.Copy),
                                 reads=[pr], writes=MT.Rc(c))
                        else:
                            S.op("dve", lambda e, ps=ps, c=c: e.tensor_tensor(out=MT.ap[:, c, :N], in0=ps[:, :N],
                                                                             in1=MT.ap[:, c, :N], op=ALU.add),
                                 reads=[pr] + MT.Rc(c), writes=MT.Rc(c))
                    if extra:
                        ps, pr = xacc
                        fv = ffs[:, mh * 4:mh * 4 + 4, :]
                        pv = ps[:, 0:64].rearrange("p (a b) -> p a b", a=4)
                        if half == 0:
                            S.op("act", lambda e: e.activation(out=fv, in_=pv, func=AF.Copy), reads=[pr], writes=["ffs"])
                        else:
                            S.op("dve", lambda e: e.tensor_tensor(out=fv, in0=pv, in1=fv, op=ALU.add),
                                 reads=[pr, "ffs"], writes=["ffs"])

        def rope_evac(ps, pr, nb, blk, dst, dregs, perm=False, h0=0, nh=6):
            k = state["tmp"]
            state["tmp"] = 1 - k
            tb = sq[k]
            if perm:
                pv = ps[:nb, 0:384].rearrange("p (kv r d) -> p kv r d", kv=2, r=3)
                cb = ropec[:nb, blk, :].unsqueeze(1).unsqueeze(1).broadcast_to([nb, 2, 3, 32])
                sbb = ropes[:nb, blk, :].unsqueeze(1).unsqueeze(1).broadcast_to([nb, 2, 3, 32])
                tv = tb[:nb, 0:192].rearrange("p (kv r d) -> p kv r d", kv=2, r=3)
                lo = lambda a: a[:, :, :, 0:32]
                hi = lambda a: a[:, :, :, 32:64]
            else:
                pv = ps[:nb, h0 * 64:(h0 + nh) * 64].rearrange("p (h d) -> p h d", h=nh)
                cb = ropec[:nb, blk, :].unsqueeze(1).broadcast_to([nb, nh, 32])
                sbb = ropes[:nb, blk, :].unsqueeze(1).broadcast_to([nb, nh, 32])
                tv = tb[:nb, 0:nh * 32].rearrange("p (h d) -> p h d", h=nh)
                lo = lambda a: a[:, :, 0:32]
                hi = lambda a: a[:, :, 32:64]
            rd = [pr, "ropec", "ropes"]
            S.op("dve", lambda e: e.tensor_tensor(out=lo(dst), in0=lo(pv), in1=cb, op=ALU.mult), reads=rd, writes=dregs)
            S.op("dve", lambda e: e.tensor_tensor(out=tv, in0=hi(pv), in1=sbb, op=ALU.mult), reads=rd, writes=[("sq", k)])
            S.op("dve", lambda e: e.tensor_tensor(out=lo(dst), in0=lo(dst), in1=tv, op=ALU.subtract),
                 reads=dregs + [("sq", k)], writes=dregs)
            S.op("dve", lambda e: e.tensor_tensor(out=hi(dst), in0=hi(pv), in1=cb, op=ALU.mult), reads=rd, writes=dregs)
            S.op("dve", lambda e: e.tensor_tensor(out=tv, in0=lo(pv), in1=sbb, op=ALU.mult),
                 reads=rd + [("sq", k)], writes=[("sq", k)])
            S.op("dve", lambda e: e.tensor_tensor(out=hi(dst), in0=hi(dst), in1=tv, op=ALU.add),
                 reads=dregs + [("sq", k)], writes=dregs)

        def phase_a_tile(ti, stage="all", extra=False):
            sample = (ti == 8)
            N = 16 if sample else NT
            if sample:
                x_fn = lambda c: xsres[:, c, :]
                xregs = ["xsres"]
                x_dst = lambda c0, c1, t0, nb: xsres[:, c0:c1, t0:t0 + nb]
                src_rows = xs
                x_blk = None
                mt_blk = None
            else:
                slot = ti % 4
                x_fn = lambda c: xres[:, c, slot * NT:(slot + 1) * NT]
                xregs = xr_regs(slot)
                x_dst = lambda c0, c1, t0, nb: xres[:, c0:c1, slot * NT + t0:slot * NT + t0 + nb]
                src_rows = xp[ti * NT:(ti + 1) * NT, :]
                x_blk = lambda c0, c1: xres[:, c0:c1, slot * NT:(slot + 1) * NT]
                mt_blk = lambda c0, c1: MT.ap[:, c0:c1, :]
            ubr = lambda c: ubreg(c, N)
            nb_fn = lambda c: UB.ap[:, c, 15:15 + N]
            if stage != "post":
                if ti == 0:
                    load_xT(xph, 16, lambda c0, c1, t0, nb: uexs[:, c0:c1, 0:16], ["uexs"])
                    rms_to(lambda c: uexs[:, c, 0:16], ["uexs"], gcol_fn(0), "gains",
                           lambda c: uexs[:, c, 16:32], ["uexs"], 16)
                    S.op("pool", lambda e: e.tensor_copy(out=UB.ap[:, :, 0:15], in_=uexs[:, :, 17:32]),
                         reads=["uexs"], writes=UB.R)
                elif not sample:
                    S.op("pool", lambda e: e.tensor_copy(out=UB.ap[:, :, 0:15], in_=halo[:]), reads=["halo"], writes=UB.R)
                drain(1)
                if ti not in preloaded:
                    load_xT(src_rows, N, x_dst, xregs)
                    preloaded.add(ti)
                if DBG.get('stop', 99) <= 1:
                    return
                if sample:
                    load_xT(cpool, 60, lambda c0, c1, t0, nb: uexs[:, c0:c1, :].rearrange(
                        "p c (s t) -> p c s t", s=4)[:, :, :, 0:15], ["uexs"])
                    rms_to(x_fn, xregs, gcol_fn(0), "gains", nb_fn, UB.R, N)
                    S.op("pool", lambda e: e.tensor_copy(
                        out=uexs[:, :, :].rearrange("p c (s t) -> p c s t", s=4)[:, :, :, 15:19],
                        in_=UB.ap[:, :, 15:31].rearrange("p c (s t) -> p c s t", s=4)), reads=UB.R, writes=["uexs"])
                    uext = lambda c0, c1: uexs[:, c0:c1, :]
                    uregs = ["uexs"]
                    E = 76
                else:
                    rms_to(x_fn, xregs, gcol_fn(0), "gains", nb_fn, ubr, N, x_blk)
                    drain(1)
                    S.op("pool", lambda e: e.tensor_copy(out=halo[:], in_=UB.ap[:, :, NT:NT + 15]), reads=UB.R, writes=["halo"])
                    uext = lambda c0, c1: UB.ap[:, c0:c1, :]
                    uregs = UB.R
                    E = NT + 15
                if DBG.get('stop', 99) <= 2:
                    return
                if ti == 7 or sample:
                    drain()
                if ti == 7:
                    for half in range(2):
                        ps, pr = psum()
                        for cc in range(4):
                            c = half * 4 + cc
                            S.op("pe", lambda e, ps=ps, cc=cc, c=c: e.transpose(
                                out=ps[:15, cc * 128:(cc + 1) * 128], in_=UB.ap[:, c, NT:NT + 15], identity=ident[:, :]),
                                reads=UB.R + ["ident"], writes=[pr], inc=(cc == 3))
                        S.op("act", lambda e, ps=ps, half=half: e.activation(
                            out=YTOK[0].flat[:15, half * 512:(half + 1) * 512], in_=ps[:15, :], func=AF.Copy),
                            reads=[pr], writes=YTOK[0].R)
                    S.dma("pool", poolp, YTOK[0].flat[:15, :], reads=YTOK[0].R, writes=["poolp"])
                if sample:
                    for half in range(2):
                        ps, pr = psum()
                        for cc in range(4):
                            c = half * 4 + cc
                            S.op("pe", lambda e, ps=ps, cc=cc, c=c: e.transpose(
                                out=ps[:16, cc * 128:(cc + 1) * 128],
                                in_=UB.ap[:, c, 15:31], identity=ident[:, :]),
                                reads=UB.R + ["ident"], writes=[pr], inc=(cc == 3))
                        S.op("act", lambda e, ps=ps, half=half: e.activation(
                            out=YTOK[0].flat[:16, half * 512:(half + 1) * 512], in_=ps[:16, :], func=AF.Copy),
                            reads=[pr], writes=YTOK[0].R)
                    for s in range(4):
                        S.dma("pool", pools[s, 0:11, :], cpool[s * 15 + 4:s * 15 + 15, :], writes=[("pools", s)])
                        S.dma("pool", pools[s, 11:15, :], YTOK[0].flat[s * 4:(s + 1) * 4, :], reads=YTOK[0].R,
                              writes=[("pools", s)])
                if DBG.get('stop', 99) <= 3:
                    return
                for g in range(4):
                    w = 2 << g
                    src = uext(2 * g, 2 * g + 2)
                    sreg = uregs
                    bufs = [PA, PBv]
                    for k in range(g + 1):
                        sh = 1 << k
                        lo = (2 << k) - 1
                        dv = bufs[k % 2]
                        dst = dv.ap[:, :, 0:E]
                        S.op("dve", lambda e, dst=dst, src=src, lo=lo, sh=sh: e.tensor_tensor(
                            out=dst[:, :, lo:E], in0=src[:, :, lo:E], in1=src[:, :, lo - sh:E - sh], op=ALU.add),
                            reads=sreg, writes=dv.R)
                        src, sreg = dst, dv.R
                    pl = PL[g % 2]
                    u2 = uext(2 * g, 2 * g + 2)
                    if sample:
                        sv = src.rearrange("p c (s t) -> p c s t", s=4)
                        uv = u2.rearrange("p c (s t) -> p c s t", s=4)
                        for cc in range(2):
                            S.op("dve", lambda e, cc=cc: e.scalar_tensor_tensor(
                                out=pl.ap[:, cc, 0:16].rearrange("p (s t) -> p s t", s=4), in0=sv[:, cc, :, 15:19],
                                scalar=1.0 / w, in1=uv[:, cc, :, 15:19], op0=ALU.mult, op1=ALU.subtract),
                                reads=sreg + uregs, writes=pl.R)
                    else:
                        S.op("dve", lambda e: e.scalar_tensor_tensor(
                            out=pl.ap[:, :, :], in0=src[:, :, 15:E], scalar=1.0 / w, in1=u2[:, :, 15:E],
                            op0=ALU.mult, op1=ALU.subtract), reads=sreg + uregs, writes=pl.R)
                        if ti in (0, 4):
                            wh = 0 if ti == 0 else 1
                            S.op("dve", lambda e: e.tensor_tensor(out=pl.ap[:, :, 0:16], in0=src[:, :, 15:31],
                                                                  in1=icnt[:, wh, 2 * g:2 * g + 2, :], op=ALU.mult),
                                 reads=sreg + ["icnt"], writes=pl.R)
                            S.op("dve", lambda e: e.tensor_tensor(out=pl.ap[:, :, 0:16], in0=pl.ap[:, :, 0:16],
                                                                  in1=u2[:, :, 15:31], op=ALU.subtract),
                                 reads=pl.R + uregs, writes=pl.R)
                    drain()
                    for eo in range(2):
                        ps, pr = psum()
                        for cc in range(2):
                            S.op("pe", lambda e, ps=ps, cc=cc, eo=eo: e.matmul(
                                ps[:, :N], lhsT=WPOOL.ap[:, g * 2 + cc, eo * 128:(eo + 1) * 128], rhs=pl.ap[:, cc, :N],
                                start=(cc == 0), stop=(cc == 1)),
                                reads=WPOOL.R + pl.R, writes=[pr], inc=(cc == 1))
                        c = 2 * g + eo
                        S.op("act", lambda e, ps=ps, c=c: e.activation(out=MT.ap[:, c, :N], in_=ps[:, :N], func=AF.Copy,
                                                                        scale=pscale[:, c:c + 1]),
                             reads=[pr, "pscale"], writes=MT.Rc(c))
                if DBG.get('stop', 99) <= 4:
                    return
                mt_fn = lambda c: MT.ap[:, c, :N]
                tl = DBG["tiles"]
                nxt = tl[tl.index(ti) + 1] if tl.index(ti) + 1 < len(tl) else None
                if nxt is not None and nxt not in preloaded:
                    if nxt == 8:
                        load_xT(xs, 16, lambda c0, c1, t0, nb: xsres[:, c0:c1, t0:t0 + nb], ["xsres"])
                    else:
                        ns = nxt % 4
                        load_xT(xp[nxt * NT:(nxt + 1) * NT, :], NT,
                                lambda c0, c1, t0, nb, ns=ns: xres[:, c0:c1, ns * NT + t0:ns * NT + t0 + nb], xr_regs(ns))
                    preloaded.add(nxt)
                rms_residual(mt_fn, MT.R, gcol_fn(1), "gains", x_fn, xregs, N, mt_blk)
                if DBG.get('stop', 99) <= 5:
                    return
            mt_fn = lambda c: MT.ap[:, c, :N]
            if stage == "pre":
                rms_to(x_fn, xregs, gcol_fn(2), "gains", lambda c: hs[:, c, :], ["hs"], N)
                return
            if stage == "post":
                drain()
                rms_residual(lambda c: ffs[:, c, :], ["ffs"], gcol_fn(3), "gains", x_fn, xregs, N)
            else:
                rms_to(x_fn, xregs, gcol_fn(2), "gains", nb_fn, ubr, N, x_blk)
                if DBG.get('stop', 99) <= 6:
                    return
                mlp(0, nb_fn, ubr, N, extra=extra)
                if DBG.get('stop', 99) <= 7:
                    return
                rms_residual(mt_fn, MT.R, gcol_fn(3), "gains", x_fn, xregs, N, mt_blk)
            if DBG.get('stop', 99) <= 8:
                return
            rms_to(x_fn, xregs, lambda c: kvg[:, c:c + 1], "kvg", mt_fn, lambda c: MT.Rc(c), N, x_blk)
            wkv = w_kv.rearrange("(kc p) f -> p kc f", p=128)
            wbs = []
            g2only = (not sample) and ti in (0, 1, 2)
            kc0 = 256 if g2only else 0
            kcn = 128 if g2only else 384
            for part in range(2):
                wb = next_wb()
                S.dma("sp", wb.ap[:, :, kc0:kc0 + kcn], wkv[:, :, part * 384 + kc0:part * 384 + kc0 + kcn], writes=wb.R)
                wbs.append(wb)
            nblk = (N + 127) // 128

            def kv_block(b):
                nb = min(128, N - b * 128)
                kst = kvst_t[b % 2]
                kreg = [("kvst", b % 2)]
                pss = []
                for part in range(2):
                    ps, pr = psum()
                    wb = wbs[part]
                    for kc in range(NCH):
                        S.op("pe", lambda e, ps=ps, wb=wb, kc=kc: e.matmul(
                            ps[:nb, kc0:kc0 + kcn], lhsT=MT.ap[:, kc, b * 128:b * 128 + nb], rhs=wb.ap[:, kc, kc0:kc0 + kcn],
                            start=(kc == 0), stop=(kc == NCH - 1)),
                            reads=MT.Rc(kc) + wb.Rc(kc), writes=[pr], inc=(kc == NCH - 1))
                    pss.append((ps, pr))
                blk = 32 if sample else ti * 4 + b
                nh = kcn // 64
                rope_evac(pss[0][0], pss[0][1], nb, blk, kst[:nb, kc0:kc0 + kcn].rearrange("p (h d) -> p h d", h=nh), kreg,
                          h0=kc0 // 64, nh=nh)
                S.op("act", lambda e: e.activation(out=kst[:nb, 384 + kc0:384 + kc0 + kcn], in_=pss[1][0][:nb, kc0:kc0 + kcn],
                                                    func=AF.Copy), reads=[pss[1][1]], writes=kreg)
                if sample:
                    S.dma("pool", kvs[2 * CH:2 * CH + 16, :], kst[:16, :], reads=kreg, writes=[("kvs", 8)])
                    S.dma("pool", kvo[CH:CH + 16, :], kst[:16, :], reads=kreg, writes=["kvo"])
                else:
                    r0 = ti * NT + b * 128
                    if g2only:
                        S.dma("pool", kvs[r0:r0 + 128, 256:384], kst[:, 256:384], reads=kreg, writes=[("kvs", ti)])
                        S.dma("pool", kvs[r0:r0 + 128, 640:768], kst[:, 640:768], reads=kreg, writes=[("kvs", ti)])
                    else:
                        S.dma("pool", kvs[r0:r0 + 128, :], kst[:, :], reads=kreg, writes=[("kvs", ti)])
                    if ti >= 4:
                        S.dma("pool", kvo[r0 - CH:r0 - CH + 128, :], kst[:, :], reads=kreg, writes=["kvo"])

            for b in range(nblk):
                deferred.append(lambda b=b: kv_block(b))

        for g in range(4):
            S.dma("sp", WPOOL.ap[:, g * 2:g * 2 + 2, :], w_pool[g].rearrange("(cc p) e -> p cc e", p=128),
                  writes=WPOOL.Rc(g * 2, g * 2 + 2))
        tl_a = list(DBG["tiles"])
        merge_a = (7 in tl_a and 8 in tl_a)
        for ti in tl_a:
            if merge_a and ti == 7:
                phase_a_tile(8, stage="pre")
                phase_a_tile(7, extra=True)
            elif merge_a and ti == 8:
                phase_a_tile(8, stage="post")
            else:
                phase_a_tile(ti)
        drain()

        def emit_y(blocks):
            for b in blocks:
                nb = 128 if b < 16 else 16
                yt = YTOK[b % 2]
                for half in range(2):
                    ps, pr = psum()
                    for cc in range(4):
                        c = half * 4 + cc
                        src = xres[:, c, b * 128:(b + 1) * 128] if b < 16 else xsres[:, c, :]
                        rd = [("xres", b // 4)] if b < 16 else ["xsres"]
                        S.op("pe", lambda e, ps=ps, cc=cc, src=src, nb=nb: e.transpose(
                            out=ps[:nb, cc * 128:(cc + 1) * 128], in_=src, identity=ident[:, :]),
                            reads=rd + ["ident"], writes=[pr], inc=(cc == 3))
                    S.op("act", lambda e, ps=ps, half=half, nb=nb, yt=yt: e.activation(
                        out=yt.flat[:nb, half * 512:(half + 1) * 512], in_=ps[:nb, :], func=AF.Copy),
                        reads=[pr], writes=yt.R)
                S.dma("pool", y_o[b * 128:b * 128 + nb, :], yt.flat[:nb, :], reads=yt.R, writes=["y"])

        pso = [(psb[6], ("ps", 6)), (psb[7], ("ps", 7))]

        def attn_block(g, nq, colsel, kblocks):
            if DBG.get('nonew') and kblocks[-1]['nk'] == 4:
                kblocks = kblocks[:-1]
            if g not in DBG.get('agroups', (0, 1, 2)):
                return
            nkb = len(kblocks)
            AS = DBG.get('astop', 99)
            for p0 in range(0, nkb, 2):
                grp = list(enumerate(kblocks))[p0:p0 + 2]
                bufs = {}
                for bi, kbk in grp:
                    i = state["kb"]
                    state["kb"] = (i + 1) % 4
                    kb, kt, va = KBLK[i], KT[i], VA[i]
                    bufs[bi] = (kt, va)
                    nk = kbk["nk"]
                    S.dma("pool", kb.flat[:nk, :], kbk["k_src"], reads=kbk["reads"], writes=kb.R)
                    S.dma("pool", va.ap[:nk, :, 0:64], kbk["v_src"].rearrange("n (h d) -> n h d", h=2),
                          reads=kbk["reads"], writes=va.R)
                    pst, ptr = psum()
                    S.op("pe", lambda e: e.transpose(out=pst[:, :nk], in_=kb.flat[:nk, :], identity=ident[:nk, :nk]),
                         reads=kb.R + ["ident"], writes=[ptr])
                    S.op("act", lambda e: e.activation(out=kt.flat[:, :nk], in_=pst[:, :nk], func=AF.Copy),
                         reads=[ptr], writes=kt.R)
                svs = {}
                for bi, kbk in grp:
                    kt, va = bufs[bi]
                    nk = kbk["nk"]
                    for kv in range(2):
                        pss, psr = psum()
                        sv = pss[:nk, 0:3 * nq].rearrange("p (r q) -> p r q", r=3)
                        S.op("pe", lambda e: e.matmul(sv, lhsT=kt.flat[kv * 64:(kv + 1) * 64, :nk],
                                                      rhs=colsel(QT.ap[kv * 64:(kv + 1) * 64, 3 * g:3 * g + 3, :]),
                                                      start=True, stop=True),
                             reads=kt.R + QT.Rc(3 * g, 3 * g + 3), writes=[psr])
                        svs[(bi, kv)] = (sv, psr)
                pbs = {}
                for bi, kbk in grp:
                    nk = kbk["nk"]
                    for kv in range(2):
                        sv, psr = svs[(bi, kv)]
                        j = state["pb"]
                        state["pb"] = (j + 1) % 4
                        pb = PB[j]
                        pv = pb.ap[:nk, :, 0:nq]
                        S.op("act", lambda e: e.activation(out=pv, in_=sv, func=AF.Exp, scale=0.125),
                             reads=[psr], writes=pb.R)
                        mk = kbk["mask"]
                        if kbk["hm"] is None:
                            S.op("dve", lambda e: e.tensor_tensor(out=pv, in0=pv, in1=mk, op=ALU.mult),
                                 reads=pb.R + ["masks"], writes=pb.R)
                        else:
                            S.op("dve", lambda e: e.scalar_tensor_tensor(out=pv, in0=pv, scalar=kbk["hm"], in1=mk,
                                                                          op0=ALU.mult, op1=ALU.mult),
                                 reads=pb.R + ["masks", "hmask"], writes=pb.R)
                        pbs[(bi, kv)] = pb
                for bi, kbk in grp:
                    kt, va = bufs[bi]
                    nk = kbk["nk"]
                    for kv in range(2):
                        pb = pbs[(bi, kv)]
                        for r in range(3):
                            S.op("pe", lambda e: e.matmul(pso[kv][0][:nq, r * 65:(r + 1) * 65], lhsT=pb.ap[:nk, r, 0:nq],
                                                          rhs=va.ap[:nk, kv, 0:65], start=(bi == 0 and r == 0),
                                                          stop=(bi == nkb - 1 and r == 2)),
                                 reads=pb.R + va.R, writes=[pso[kv][1]], inc=(r == 2))
            if AS <= 4:
                return
            for kv in range(2):
                ps, pr = pso[kv]
                v3 = ps[:nq, 0:195].rearrange("p (r c) -> p r c", r=3)
                S.op("act", lambda e: e.activation(
                    out=OSN.ap[:nq, kv * 3:(kv + 1) * 3, :], in_=v3[:, :, 0:64],
                    func=AF.Copy), reads=[pr], writes=OSN.R)
                S.op("act", lambda e: e.activation(
                    out=OSD.ap[:nq, kv * 3:(kv + 1) * 3, :],
                    in_=v3[:, :, 64:65].broadcast_to([nq, 3, 64]),
                    func=AF.Copy), reads=[pr], writes=OSD.R)
            if AS <= 5:
                return
            for src, dview, is_den in ((OSN, OT, False), (OSD, DT, True)):
                pst, ptr = psum()
                for r in range(3):
                    tin = src.flat[:nq, r * 128:(r + 1) * 128]
                    S.op("pe", lambda e: e.transpose(out=pst[:, r * 128:r * 128 + nq], in_=tin, identity=ident[:nq, :nq]),
                         reads=src.R + ["ident"], writes=[ptr], inc=(r == 2))
                pv3 = pst[:, 0:384].rearrange("p (r q) -> p r q", r=3)[:, :, 0:nq]
                r0 = 0 if is_den else 3 * g
                dst = colsel(dview.ap[:, r0:r0 + 3, :])
                dreg = dview.Rc(r0, r0 + 3)
                if not is_den:
                    S.op("act", lambda e: e.activation(out=dst, in_=pv3, func=AF.Copy), reads=[ptr], writes=dreg)
                elif g == 0:
                    S.op("act", lambda e: e.activation(out=dst, in_=pv3, func=AF.Copy), reads=[ptr], writes=dreg)
                else:
                    S.op("dve", lambda e: e.tensor_tensor(out=dst, in0=pv3, in1=dst, op=ALU.add),
                         reads=[ptr] + dreg, writes=dreg)

        def phase_b_tile(T, stage="all", extra=False):
            sample = (T == 4)
            N = 16 if sample else NT
            if sample:
                x_fn = lambda c: xsres[:, c, :]
                xregs = ["xsres"]
            else:
                x_fn = lambda c: xres[:, c, T * NT:(T + 1) * NT]
                xregs = xr_regs(T)
            x_blk = None if sample else (lambda c0, c1: xres[:, c0:c1, T * NT:(T + 1) * NT])
            mt_blk = None if sample else (lambda c0, c1: MT.ap[:, c0:c1, :])
            ubr = lambda c: ubreg(c, N)
            nb_fn = lambda c: UB.ap[:, c, 15:15 + N]
            mt_fn = lambda c: MT.ap[:, c, :N]
            if stage != "post":
                rms_to(x_fn, xregs, gcol_fn(4), "gains", nb_fn, ubr, N, x_blk)
                wq = w_q.rearrange("(kc p) f -> p kc f", p=128)
                nblk = (N + 127) // 128
                pend = []
                for part in range(3):
                    g = part
                    wb = next_wb()
                    S.dma("sp", wb.ap[:, :, 0:384], wq[:, :, part * 384:(part + 1) * 384], writes=wb.R)
                    for b in range(nblk):
                        nb = min(128, N - b * 128)
                        ps, pr = psum()
                        for kc in range(NCH):
                            S.op("pe", lambda e: e.matmul(ps[:nb, 0:384], lhsT=UB.ap[:, kc, 15 + b * 128:15 + b * 128 + nb],
                                                          rhs=wb.ap[:, kc, 0:384], start=(kc == 0), stop=(kc == NCH - 1)),
                                 reads=ubr(kc) + wb.Rc(kc), writes=[pr], inc=(kc == NCH - 1))
                        while pend:
                            pend.pop(0)()
                        blk = 32 if sample else 16 + T * 4 + b
                        slot = (part * nblk + b) % 3
                        qv = QST.flat[:nb, slot * 384:(slot + 1) * 384]
                        qr = QST.Rr(slot * 384, (slot + 1) * 384)
                        rope_evac(ps, pr, nb, blk, qv.rearrange("p (r kv d) -> p kv r d", r=3, kv=2), qr, perm=True)

                        def q_transposes(g=g, b=b, nb=nb, qv=qv, qr=qr):
                            ps2, pr2 = psum()
                            for r in range(3):
                                S.op("pe", lambda e: e.transpose(out=ps2[:, r * 128:r * 128 + nb],
                                                                 in_=qv[:, r * 128:(r + 1) * 128], identity=ident[:nb, :nb]),
                                     reads=qr + ["ident"], writes=[pr2], inc=(r == 2))
                            S.op("act", lambda e: e.activation(
                                out=QT.ap[:, 3 * g:3 * g + 3, b * 128:b * 128 + nb],
                                in_=ps2[:, 0:384].rearrange("p (r q) -> p r q", r=3)[:, :, :nb], func=AF.Copy),
                                reads=[pr2], writes=QT.Rc(3 * g, 3 * g + 3))
                        pend.append(q_transposes)
                while pend:
                    pend.pop(0)()
                if DBG.get('bstop', 99) <= 2:
                    return
                for i in range(4):
                    S.op("dve", lambda e: e.memset(VA[i].ap[:, :, 64:128], 1.0), writes=VA[i].R)
                if not sample:
                    R0 = CH + NT * T
                    kvr = [("kvs", t) for t in range(8)]
                    for g in range(3):
                        kc0, vc0 = g * 128, 384 + g * 128
                        if g == 0:
                            for i in range(4):
                                p0 = R0 - 128 + 128 * i
                                c0 = R0 + 128 * i
                                kbs = [dict(k_src=kvs[p0:p0 + 128, kc0:kc0 + 128], v_src=kvs[p0:p0 + 128, vc0:vc0 + 128], nk=128,
                                            mask=maskA[:, :, :], hm=(hmask[:, 0:1] if (T == 0 and i == 0) else None), reads=kvr),
                                       dict(k_src=kvs[c0:c0 + 128, kc0:kc0 + 128], v_src=kvs[c0:c0 + 128, vc0:vc0 + 128], nk=128,
                                            mask=maskB[:, :, :], hm=None, reads=kvr)]
                                attn_block(0, 128, lambda a, i=i: a[:, :, i * 128:(i + 1) * 128], kbs)
                        elif g == 1:
                            k4 = kvs.rearrange("(a s) c -> a s c", s=4)
                            for rr in range(4):
                                ap0 = (R0 - NT) // 4
                                ac0 = R0 // 4
                                kbs = [dict(k_src=k4[ap0:ap0 + 128, rr, kc0:kc0 + 128], v_src=k4[ap0:ap0 + 128, rr, vc0:vc0 + 128],
                                            nk=128, mask=maskA[:, :, :], hm=(hmask[:, 0:1] if T == 0 else None), reads=kvr),
                                       dict(k_src=k4[ac0:ac0 + 128, rr, kc0:kc0 + 128], v_src=k4[ac0:ac0 + 128, rr, vc0:vc0 + 128],
                                            nk=128, mask=maskB[:, :, :], hm=None, reads=kvr)]
                                attn_block(1, 128, lambda a, rr=rr: a.rearrange("p r (q s) -> p r q s", s=4)[:, :, :, rr], kbs)
                        else:
                            k16 = kvs.rearrange("(a s) c -> a s c", s=16)
                            for rr in range(16):
                                aa0 = (NT * T) // 16
                                ab0 = R0 // 16
                                kbs = [dict(k_src=k16[aa0:aa0 + 128, rr, kc0:kc0 + 128], v_src=k16[aa0:aa0 + 128, rr, vc0:vc0 + 128],
                                            nk=128, mask=maskA[:, :, 0:32], hm=hmask[:, T:T + 1], reads=kvr),
                                       dict(k_src=k16[ab0:ab0 + 32, rr, kc0:kc0 + 128], v_src=k16[ab0:ab0 + 32, rr, vc0:vc0 + 128],
                                            nk=32, mask=maskB[0:32, :, 0:32], hm=None, reads=kvr)]
                                attn_block(2, 32, lambda a, rr=rr: a.rearrange("p r (q s) -> p r q s", s=16)[:, :, :, rr], kbs)
                else:
                    newr = [("kvs", 8)]
                    for g in range(3):
                        kc0, vc0 = g * 128, 384 + g * 128
                        for sq_ in range(4):
                            n0 = 2 * CH + 4 * sq_
                            newb_k = kvs[n0:n0 + 4, kc0:kc0 + 128]
                            newb_v = kvs[n0:n0 + 4, vc0:vc0 + 128]
                            if g == 0:
                                kbs = [dict(k_src=ck[sq_, 1920:2048, 0:128], v_src=cv[sq_, 1920:2048, 0:128], nk=128,
                                            mask=msamp[:, 0], hm=None, reads=[]),
                                       dict(k_src=newb_k, v_src=newb_v, nk=4, mask=msamp[0:4, 5], hm=None, reads=newr)]
                            else:
                                st_ = 4 if g == 1 else 16
                                a0 = 384 if g == 1 else 0
                                ckr = ck[sq_].rearrange("(a t) c -> a t c", t=st_)
                                cvr = cv[sq_].rearrange("(a t) c -> a t c", t=st_)
                                kbs = [dict(k_src=ckr[a0:a0 + 128, i, kc0:kc0 + 128], v_src=cvr[a0:a0 + 128, i, kc0:kc0 + 128],
                                            nk=128, mask=msamp[:, 1 + i], hm=None, reads=[]) for i in range(4)]
                                kbs.append(dict(k_src=newb_k, v_src=newb_v, nk=4, mask=msamp[0:4, 6], hm=None, reads=newr))
                            attn_block(g, 4, lambda a, sq_=sq_: a[:, :, sq_ * 4:(sq_ + 1) * 4], kbs)
                if DBG.get('bstop', 99) <= 4:
                    return
                for r in range(3):
                    S.op("dve", lambda e: e.reciprocal(out=DT.ap[:, r, :N], in_=DT.ap[:, r, :N]), reads=DT.Rc(r), writes=DT.Rc(r))
                for g in range(3):
                    for r in range(3):
                        c = 3 * g + r
                        S.op("pool", lambda e: e.tensor_tensor(out=OT.ap[:, c, :N], in0=OT.ap[:, c, :N], in1=DT.ap[:, r, :N],
                                                               op=ALU.mult), reads=OT.Rc(c) + DT.Rc(r), writes=OT.Rc(c))
                if DBG.get('bstop', 99) <= 5:
                    return
                wo3 = w_o.rearrange("(c p) m -> p c m", p=128)
                for qtr in range(4):
                    wb = next_wb()
                    wv = wb.flat[:, 0:9 * 256].rearrange("p (c m) -> p c m", c=9)
                    S.dma("sp", wv, wo3[:, :, qtr * 256:(qtr + 1) * 256], writes=wb.R)
                    for mm in range(2):
                        ps, pr = psum()
                        for kc in range(9):
                            S.op("pe", lambda e: e.matmul(ps[:, :N], lhsT=wv[:, kc, mm * 128:(mm + 1) * 128], rhs=OT.ap[:, kc, :N],
                                                          start=(kc == 0), stop=(kc == 8)),
                                 reads=wb.R + OT.Rc(kc), writes=[pr], inc=(kc == 8))
                        c = qtr * 2 + mm
                        S.op("act", lambda e: e.activation(out=MT.ap[:, c, :N], in_=ps[:, :N], func=AF.Copy),
                             reads=[pr], writes=MT.Rc(c))
                if DBG.get('bstop', 99) <= 6:
                    return
            if stage != "post":
                rms_residual(mt_fn, MT.R, gcol_fn(5), "gains", x_fn, xregs, N, mt_blk)
            if stage == "pre":
                rms_to(x_fn, xregs, gcol_fn(6), "gains", lambda c: hs[:, c, :], ["hs"], N)
                return
            if stage == "post":
                rms_residual(lambda c: ffs[:, c, :], ["ffs"], gcol_fn(7), "gains", x_fn, xregs, N)
            else:
                rms_to(x_fn, xregs, gcol_fn(6), "gains", nb_fn, ubr, N, x_blk)
                mlp(1, nb_fn, ubr, N, extra=extra)
                rms_residual(mt_fn, MT.R, gcol_fn(7), "gains", x_fn, xregs, N, mt_blk)
            emit_y([16] if sample else list(range(T * 4, T * 4 + 4)))

        if do_phase_b:
            tl_b = list(DBG.get("btiles", list(range(5))))
            merge_b = (3 in tl_b and 4 in tl_b)
            for T in tl_b:
                if merge_b and T == 3:
                    phase_b_tile(4, stage="pre")
                    phase_b_tile(3, extra=True)
                elif merge_b and T == 4:
                    phase_b_tile(4, stage="post")
                else:
                    phase_b_tile(T)
        elif DBG["emit_y"]:
            emit_y(list(range(17)))
        S.finish()
        print("instructions", S.n_inst, "waits", S.n_wait)
    return nc


def _rope_inv():
    try:
        import jax
        import jax.numpy as jnp
        with jax.default_device(jax.devices("cpu")[0]):
            v = 10000.0 ** (-jnp.arange(0, HD, 2, dtype=jnp.float32) / HD)
            return np.asarray(v, dtype=np.float32)
    except Exception:
        e = (-np.arange(0, HD, 2, dtype=np.float32) / np.float32(HD)).astype(np.float32)
        return np.power(np.float32(10000.0), e).astype(np.float32)


def _const_tables(q):
    inv = _rope_inv()
    pos = np.concatenate([(q - 1) * CH + np.arange(2 * CH), PAST + np.tile(np.arange(4), 4)]).astype(np.float32)
    ang = (pos[:, None] * inv[None, :]).astype(np.float32).astype(np.float64)
    cos = np.cos(ang).astype(np.float32)
    sin = np.sin(ang).astype(np.float32)
    rc = np.zeros((33 * 128, 32), np.float32)
    rsn = np.zeros((33 * 128, 32), np.float32)
    rc[:2 * CH + 16] = cos
    rsn[:2 * CH + 16] = sin
    rc = rc.reshape(33, 128, 32).transpose(1, 0, 2).reshape(128, 33 * 32)
    rsn = rsn.reshape(33, 128, 32).transpose(1, 0, 2).reshape(128, 33 * 32)
    return np.ascontiguousarray(rc), np.ascontiguousarray(rsn)


def _icnt(q):
    t = np.zeros((2, 8, 16), np.float32)
    for c in range(8):
        w = 2 << (c // 2)
        t[:, c, :] = 1.0 / w
    posn = np.arange(16)
    for c in range(8):
        w = 2 << (c // 2)
        tab = 1.0 / np.minimum(w, posn + 1).astype(np.float32)
        if q == 1:
            t[0, c, :] = tab
        if q == 0:
            t[1, c, :] = tab
    return np.ascontiguousarray(np.broadcast_to(t.reshape(1, 256), (128, 256))).astype(np.float32)


_PROG = {}


def kernel(x_prompt, x_sample, cache_pool, cache_k, cache_v, norm_gains, kv_norm_gain, w_pool,
           pool_scale, w_q, w_o, w_kv, w_up, w_down, _phase_b=True):
    f32 = np.float32
    x_prompt = np.asarray(x_prompt, f32)
    x_sample = np.asarray(x_sample, f32)
    cache_pool = np.asarray(cache_pool, f32)
    cache_k = np.asarray(cache_k, f32)
    cache_v = np.asarray(cache_v, f32)
    key = bool(_phase_b)
    if key not in _PROG:
        _PROG[key] = build_program(do_phase_b=_phase_b)
    nc = _PROG[key]
    ncores = DBG.get("ncores", 8)

    gl = np.ascontiguousarray(np.asarray(norm_gains, f32).reshape(8, 8, 128).transpose(2, 0, 1).reshape(128, 64))
    kvgl = np.ascontiguousarray(np.asarray(kv_norm_gain, f32).reshape(8, 128).T)
    psl = np.ascontiguousarray(np.asarray(pool_scale, f32).reshape(8, 128).T)
    kk = np.arange(128)[:, None]
    qq = np.arange(128)[None, :]
    mA = np.ascontiguousarray(np.broadcast_to((kk >= qq).astype(f32)[:, None, :], (128, 3, 128))).reshape(128, 384)
    mB = np.ascontiguousarray(np.broadcast_to((kk <= qq).astype(f32)[:, None, :], (128, 3, 128))).reshape(128, 384)
    ms = np.zeros((128, 7, 3, 4), f32)
    i4 = np.arange(4)[None, :]
    ms[:, 0] = (kk >= i4).astype(f32)[:, None, :]
    for i in range(4):
        ms[:, 1 + i, :, i] = 1.0
    ms[:, 5] = (kk <= i4).astype(f32)[:, None, :]
    ms[:, 6] = (kk == i4).astype(f32)[:, None, :]
    ms = ms.reshape(128, 84)
    ident = np.eye(128, dtype=f32)
    shared = {
        "gains": gl, "kvg": kvgl, "pscale": psl, "maskA": mA, "maskB": mB, "msamp": ms, "ident": ident,
        "w_pool": np.ascontiguousarray(np.asarray(w_pool, f32)[0]),
        "w_q": np.ascontiguousarray(np.asarray(w_q, f32)[0]),
        "w_o": np.ascontiguousarray(np.asarray(w_o, f32)[0]),
        "w_kv": np.ascontiguousarray(np.asarray(w_kv, f32)),
        "w_up": np.ascontiguousarray(np.asarray(w_up, f32)),
        "w_down": np.ascontiguousarray(np.asarray(w_down, f32)),
    }
    in_maps = []
    for c in range(8):
        b, q = divmod(c, 4)
        xpc = np.zeros((2 * CH, D), f32)
        xpc[CH:] = x_prompt[b, q * CH:(q + 1) * CH]
        xphc = np.zeros((16, D), f32)
        if q >= 1:
            xpc[:CH] = x_prompt[b, (q - 1) * CH:q * CH]
        if q >= 2:
            xphc[:] = x_prompt[b, (q - 1) * CH - 16:(q - 1) * CH]
        rc, rsn = _const_tables(q)
        hm = np.ones((128, 4), f32)
        if q == 0:
            for T in range(4):
                hm[:, T] = ((32 * T - 128 + np.arange(128)) >= 0).astype(f32)
        m = dict(shared)
        m.update({
            "xp": xpc, "xph": xphc,
            "xs": np.ascontiguousarray(x_sample[4 * c:4 * c + 4].reshape(16, D)),
            "cpool": np.ascontiguousarray(cache_pool[0, 4 * c:4 * c + 4].reshape(60, D)),
            "ck": np.ascontiguousarray(cache_k[4 * c:4 * c + 4].reshape(4, 2048, 384)),
            "cv": np.ascontiguousarray(cache_v[4 * c:4 * c + 4].reshape(4, 2048, 384)),
            "icnt": _icnt(q), "ropec": rc, "ropes": rsn, "hmask": hm,
        })
        in_maps.append(m)

    res = run_bass_kernel_spmd(nc, in_maps[:ncores], core_ids=list(range(ncores)))
    R = res.results
    y_prompt = np.zeros((2, SEQ, D), f32)
    y_sample = np.zeros((32, 4, D), f32)
    pool_prompt = np.zeros((1, 2, 15, D), f32)
    k_prompt = np.zeros((2, 2048, 6, 64), f32)
    v_prompt = np.zeros((2, 2048, 6, 64), f32)
    pool_sample = np.zeros((1, 32, 15, D), f32)
    k_s = np.zeros((32, 4, 6, 64), f32)
    v_s = np.zeros((32, 4, 6, 64), f32)
    for c in range(ncores):
        b, q = divmod(c, 4)
        r = R[c]
        y_prompt[b, q * CH:(q + 1) * CH] = r["y"][:CH]
        y_sample[4 * c:4 * c + 4] = r["y"][CH:CH + 16].reshape(4, 4, D)
        pool_sample[0, 4 * c:4 * c + 4] = r["pools"]
        k_s[4 * c:4 * c + 4] = r["kvo"][CH:CH + 16, :384].reshape(4, 4, 6, 64)
        v_s[4 * c:4 * c + 4] = r["kvo"][CH:CH + 16, 384:].reshape(4, 4, 6, 64)
        if q == 3:
            pool_prompt[0, b] = r["poolp"]
            k_prompt[b] = r["kvo"][:CH, :384].reshape(2048, 6, 64)
            v_prompt[b] = r["kvo"][:CH, 384:].reshape(2048, 6, 64)
    return (y_prompt, y_sample, pool_prompt, k_prompt, v_prompt, pool_sample, k_s, v_s)
```
